# Optimizing a Trainium2 kernel written in Bass

```python
import math
import jax, jax.numpy as jnp
from jax import lax
import numpy as np

D_MODEL = 1024
BATCH = 16
SEQ = 256
DEPTH = 2
DEC_BATCH = 4
DEC_SEQ = 1024
PAST_LEN = 256

GRID_W = 64
D_HYENA = 256
D_SCONV = 256
N_HEADS = 8
QK_NOPE = 64
QK_ROPE = 32
QK_DIM = QK_NOPE + QK_ROPE
V_DIM = 64
D_MLA = N_HEADS * V_DIM
D_MIX = D_HYENA + D_SCONV + D_MLA
Q_LORA = 256
KV_LORA = 128
ROPE_THETA = 10000.0
HY_EMB = 33
HY_BANDS = (HY_EMB - 1) // 2
HY_FF = 64
HY_FAST_DECAY = 0.3
HY_SLOW_DECAY = 1.5
HY_TARGET = 1e-2
HY_SHIFT = 0.05
D_FF = ((8 * D_MODEL) // 3 + 255) // 256 * 256
N_IN = 3 * D_HYENA + 3 * D_SCONV + Q_LORA + KV_LORA + QK_ROPE
EPS = 1e-6
Q_BLOCK = 128

kernel_name = 'hybrid_hyena_sconv_mla_diffusion_step'


def rmsnorm(x, g):
    xf = x.astype(jnp.float32)
    y = xf * lax.rsqrt(jnp.mean(xf * xf, axis=-1, keepdims=True) + EPS)
    return (y * g.astype(jnp.float32)).astype(x.dtype)


def conv3(x, w):
    xp = jnp.pad(x, ((0, 0), (1, 1), (0, 0)))
    return xp[:, :-2] * w[0] + xp[:, 1:-1] * w[1] + xp[:, 2:] * w[2]


def axial_rope_tables(L):
    rows = L // GRID_W
    row = jnp.repeat(jnp.arange(rows, dtype=jnp.float32), GRID_W)
    col = jnp.tile(jnp.arange(GRID_W, dtype=jnp.float32), rows)
    half = QK_ROPE // 2
    inv = 1.0 / (ROPE_THETA ** (jnp.arange(0, half, 2, dtype=jnp.float32) / half))
    ang = jnp.concatenate([row[:, None] * inv, col[:, None] * inv], axis=-1)
    return jnp.cos(ang), jnp.sin(ang)


def apply_rope(x, cos, sin):
    Bn, L, H, _ = x.shape
    q4 = QK_ROPE // 4
    xf = x.astype(jnp.float32)
    pe = xf[..., QK_NOPE:].reshape(Bn, L, H, 2, 2, q4)
    c = cos.reshape(L, 2, q4)[None, :, None]
    s = sin.reshape(L, 2, q4)[None, :, None]
    a, b = pe[..., 0, :], pe[..., 1, :]
    rot = jnp.stack([a * c - b * s, b * c + a * s], axis=-2).reshape(Bn, L, H, QK_ROPE)
    return jnp.concatenate([xf[..., :QK_NOPE], rot], axis=-1).astype(x.dtype)


def hyena_filter_fft(L, w1, b1, freq, w2, b2, w3):
    f32 = jnp.float32
    t = jnp.linspace(0.0, 1.0, L, dtype=f32)[:, None]
    w_ang = 2.0 * math.pi * jnp.arange(L, dtype=f32)[:, None] / L
    bands = jnp.linspace(1e-4, HY_BANDS - 1, HY_BANDS, dtype=f32)[None, :]
    z = jnp.concatenate([t, jnp.cos(bands * w_ang), -jnp.sin(bands * w_ang)], axis=-1)
    fr = freq.astype(f32)
    h = jnp.sin(fr * (z @ w1.astype(f32) + b1.astype(f32)))
    h = jnp.sin(fr * (h @ w2.astype(f32) + b2.astype(f32)))
    h = h @ w3.astype(f32)
    deltas = jnp.abs(jnp.linspace(math.log(HY_TARGET) / HY_FAST_DECAY,
                                  math.log(HY_TARGET) / HY_SLOW_DECAY, D_HYENA, dtype=f32))
    window = jnp.exp(-t * deltas[None, :]) + HY_SHIFT
    h_f = h[:, :D_HYENA] * window
    h_b = h[:, D_HYENA:] * window
    k = jnp.concatenate([h_f, jnp.zeros((1, D_HYENA), f32), h_b[:0:-1]], axis=0)
    return jnp.fft.rfft(k, axis=0)


def hyena_mixer(u, conv_w, w1, b1, freq, w2, b2, w3, bias):
    L = u.shape[1]
    u = conv3(u, conv_w)
    x0, x1, v = jnp.split(u, 3, axis=-1)
    kf = hyena_filter_fft(L, w1, b1, freq, w2, b2, w3)
    z = (x1 * v).astype(jnp.float32)
    Z = jnp.fft.rfft(z, n=2 * L, axis=1)
    y = jnp.fft.irfft(Z * kf[None], n=2 * L, axis=1)[:, :L]
    y = y + z * bias.astype(jnp.float32)
    return (x0.astype(jnp.float32) * y).astype(u.dtype)


def sconv_mixer(u, conv_w):
    bg, cg, hx = jnp.split(u, 3, axis=-1)
    return bg * conv3(cg * hx, conv_w)


def mla_q(c_q, g_q, w_uq, g_qh):
    Bn, L, _ = c_q.shape
    q = (rmsnorm(c_q, g_q) @ w_uq).reshape(Bn, L, N_HEADS, QK_DIM)
    return rmsnorm(q, g_qh)


def mla_kv(c_kv, k_pe, g_kv, w_ukv, g_kh):
    Bn, L, _ = c_kv.shape
    kv = (rmsnorm(c_kv, g_kv) @ w_ukv).reshape(Bn, L, N_HEADS, QK_NOPE + V_DIM)
    k_nope, v = kv[..., :QK_NOPE], kv[..., QK_NOPE:]
    k = jnp.concatenate([k_nope, jnp.broadcast_to(k_pe[:, :, None, :], (Bn, L, N_HEADS, QK_ROPE))], axis=-1)
    return rmsnorm(k, g_kh), v


def attention(q, k, v):
    Bn, Lq, H, Dq = q.shape
    nb = Lq // Q_BLOCK
    kf = k.astype(jnp.float32)
    vf = v.astype(jnp.float32)
    qb = q.astype(jnp.float32).reshape(Bn, nb, Q_BLOCK, H, Dq).swapaxes(0, 1)
    scale = QK_DIM ** -0.5

    def one_block(qi):
        s = jnp.einsum('bqhd,bkhd->bhqk', qi, kf) * scale
        p = jax.nn.softmax(s, axis=-1)
        return jnp.einsum('bhqk,bkhd->bqhd', p, vf)

    o = lax.map(one_block, qb).swapaxes(0, 1).reshape(Bn, Lq, H * V_DIM)
    return o.astype(v.dtype)


def trunk_layer(x, mod, P, l, ctx, rope):
    sh1, sc1, g1, sh2, sc2, g2 = jnp.split(mod, 6, axis=-1)
    h = rmsnorm(x, P['g_norm1'][l]) * (1 + sc1) + sh1
    proj = h @ P['w_in'][l]
    o0 = 3 * D_HYENA
    o1 = o0 + 3 * D_SCONV
    o2 = o1 + Q_LORA
    o3 = o2 + KV_LORA
    u_hy, u_sc = proj[..., :o0], proj[..., o0:o1]
    c_q, c_kv, k_pe = proj[..., o1:o2], proj[..., o2:o3], proj[..., o3:]

    y_hy = hyena_mixer(u_hy, P['hy_conv'][l], P['hy_w1'][l], P['hy_b1'][l], P['hy_freq'][l],
                       P['hy_w2'][l], P['hy_b2'][l], P['hy_w3'][l], P['hy_bias'][l])
    y_sc = sconv_mixer(u_sc, P['sc_conv'][l])

    q = mla_q(c_q, P['g_q'][l], P['w_uq'][l], P['g_qh'][l])
    k, v = mla_kv(c_kv, k_pe, P['g_kv'][l], P['w_ukv'][l], P['g_kh'][l])
    if rope is not None:
        q = apply_rope(q, rope[0], rope[1])
        k = apply_rope(k, rope[0], rope[1])
    if ctx is not None:
        ck, cv = mla_kv(ctx[0], ctx[1], P['g_kv'][l], P['w_ukv'][l], P['g_kh'][l])
        k = jnp.concatenate([k, ck], axis=1)
        v = jnp.concatenate([v, cv], axis=1)
    y_at = attention(q, k, v)

    gg = P['g_grp'][l]
    y = jnp.concatenate([rmsnorm(y_hy, gg[:D_HYENA]),
                         rmsnorm(y_sc, gg[D_HYENA:D_HYENA + D_SCONV]),
                         rmsnorm(y_at, gg[D_HYENA + D_SCONV:])], axis=-1)
    x = x + g1 * (y @ P['w_out'][l])

    h2 = rmsnorm(x, P['g_norm2'][l]) * (1 + sc2) + sh2
    ff = (jax.nn.silu(h2 @ P['w_ff_gate'][l]) * (h2 @ P['w_ff_up'][l])) @ P['w_ff_down'][l]
    x = x + g2 * ff
    return x, c_kv, k_pe


def setup_inputs(seed: int = 0) -> dict:
    key = jax.random.key(seed)
    ks = iter(jax.random.split(key, 40))
    f32 = jnp.float32

    def nrm(shape, scale):
        return jax.random.normal(next(ks), shape, f32) * scale

    def gain(shape):
        return 1.0 + nrm(shape, 0.01)

    return {
        'x_prompt': nrm((BATCH, SEQ, D_MODEL), 1.0),
        'x_sample': nrm((DEC_BATCH, DEC_SEQ, D_MODEL), 1.0),
        'cache_ckv': nrm((DEC_BATCH, DEPTH, PAST_LEN, KV_LORA), 1.0),
        'cache_kpe': nrm((DEC_BATCH, DEPTH, PAST_LEN, QK_ROPE), 1.0),
        'c': nrm((DEC_BATCH, D_MODEL), 1.0),
        'c_ctx': nrm((D_MODEL,), 1.0),
        'w_mod': nrm((DEPTH, D_MODEL, 6 * D_MODEL), 0.3 * D_MODEL ** -0.5),
        'b_mod': nrm((DEPTH, 6 * D_MODEL), 0.01),
        'g_norm1': gain((DEPTH, D_MODEL)),
        'w_in': nrm((DEPTH, D_MODEL, N_IN), D_MODEL ** -0.5),
        'hy_conv': nrm((DEPTH, 3, 3 * D_HYENA), 3 ** -0.5),
        'hy_w1': nrm((DEPTH, HY_EMB, HY_FF), HY_EMB ** -0.5),
        'hy_b1': nrm((DEPTH, HY_FF), 0.01),
        'hy_freq': 1.0 + nrm((DEPTH, HY_FF), 0.1),
        'hy_w2': nrm((DEPTH, HY_FF, HY_FF), HY_FF ** -0.5),
        'hy_b2': nrm((DEPTH, HY_FF), 0.01),
        'hy_w3': nrm((DEPTH, HY_FF, 2 * D_HYENA), HY_FF ** -0.5),
        'hy_bias': nrm((DEPTH, D_HYENA), 0.5),
        'sc_conv': nrm((DEPTH, 3, D_SCONV), 3 ** -0.5),
        'g_q': gain((DEPTH, Q_LORA)),
        'w_uq': nrm((DEPTH, Q_LORA, N_HEADS * QK_DIM), Q_LORA ** -0.5),
        'g_kv': gain((DEPTH, KV_LORA)),
        'w_ukv': nrm((DEPTH, KV_LORA, N_HEADS * (QK_NOPE + V_DIM)), KV_LORA ** -0.5),
        'g_qh': gain((DEPTH, QK_DIM)),
        'g_kh': gain((DEPTH, QK_DIM)),
        'g_grp': gain((DEPTH, D_MIX)),
        'w_out': nrm((DEPTH, D_MIX, D_MODEL), D_MIX ** -0.5),
        'g_norm2': gain((DEPTH, D_MODEL)),
        'w_ff_gate': nrm((DEPTH, D_MODEL, D_FF), D_MODEL ** -0.5),
        'w_ff_up': nrm((DEPTH, D_MODEL, D_FF), D_MODEL ** -0.5),
        'w_ff_down': nrm((DEPTH, D_FF, D_MODEL), D_FF ** -0.5),
    }


def reference(x_prompt, x_sample, cache_ckv, cache_kpe, c, c_ctx, w_mod, b_mod, g_norm1, w_in,
              hy_conv, hy_w1, hy_b1, hy_freq, hy_w2, hy_b2, hy_w3, hy_bias, sc_conv,
              g_q, w_uq, g_kv, w_ukv, g_qh, g_kh, g_grp, w_out, g_norm2,
              w_ff_gate, w_ff_up, w_ff_down):
    P = {'g_norm1': g_norm1, 'w_in': w_in, 'hy_conv': hy_conv, 'hy_w1': hy_w1, 'hy_b1': hy_b1,
         'hy_freq': hy_freq, 'hy_w2': hy_w2, 'hy_b2': hy_b2, 'hy_w3': hy_w3, 'hy_bias': hy_bias,
         'sc_conv': sc_conv, 'g_q': g_q, 'w_uq': w_uq, 'g_kv': g_kv, 'w_ukv': w_ukv,
         'g_qh': g_qh, 'g_kh': g_kh, 'g_grp': g_grp, 'w_out': w_out, 'g_norm2': g_norm2,
         'w_ff_gate': w_ff_gate, 'w_ff_up': w_ff_up, 'w_ff_down': w_ff_down}

    xp = x_prompt
    ckv_list = []
    kpe_list = []
    for l in range(DEPTH):
        mod = (jax.nn.silu(c_ctx) @ w_mod[l] + b_mod[l])[None, None, :]
        xp, ckv, kpe = trunk_layer(xp, mod, P, l, None, None)
        ckv_list.append(ckv)
        kpe_list.append(kpe)
    new_ckv = jnp.stack(ckv_list, axis=1)
    new_kpe = jnp.stack(kpe_list, axis=1)

    xs = x_sample
    rope = axial_rope_tables(x_sample.shape[1])
    for l in range(DEPTH):
        mod = (jax.nn.silu(c) @ w_mod[l] + b_mod[l])[:, None, :]
        xs, _, _ = trunk_layer(xs, mod, P, l, (cache_ckv[:, l], cache_kpe[:, l]), rope)

    return (xp, xs, new_ckv, new_kpe)
```

```python
import numpy as np
import ml_dtypes
from contextlib import ExitStack
import concourse.bass as bass
import concourse.mybir as mybir
from concourse.bass_utils import run_bass_kernel_spmd

F32 = mybir.dt.float32
BF16 = mybir.dt.bfloat16
AF = mybir.ActivationFunctionType
ALU = mybir.AluOpType

N_DMA_SEMS = 40
N_SW_SEMS = 12
D = 1024
NT = 1024
NK = 1280
DEPTH = 2
NIN = 1952
DFF = 2816
EPS = 1e-6
BIG = 30000.0
TT = [(0, 512), (512, 1024)]
KT = [(0, 512), (512, 1024), (1024, 1280)]


class Buf:
    __slots__ = ("key", "lo", "hi")

    def __init__(self, key, lo=0, hi=1 << 30):
        self.key, self.lo, self.hi = key, lo, hi


class _Op:
    __slots__ = ("eng", "fn", "deps", "signal", "sigval", "dma", "dsem", "dval", "idx")


class Prog:
    ENGS = ("pe", "act", "dve", "pool", "sp")

    def __init__(self):
        self.ops = []
        self.writes = {}
        self.reads = {}
        self.dma_count = 0
        self.dma_count_sw = 0
        self.dma_last = {}

    def _deps_for(self, op, reads, writes):
        deps = []
        is_dma = op.dma
        for b in reads:
            for (lo, hi, w) in self.writes.get(b.key, ()):
                if lo < b.hi and b.lo < hi:
                    deps.append(w)
            if b.key.startswith("ps"):
                for (lo, hi, r) in self.reads.get(b.key, ()):
                    if r.eng != op.eng:
                        deps.append(r)
        for b in writes:
            for (lo, hi, w) in self.writes.get(b.key, ()):
                if lo < b.hi and b.lo < hi:
                    if is_dma or w.dma or w.eng != op.eng or op.eng != "pe":
                        deps.append(w)
            for (lo, hi, r) in self.reads.get(b.key, ()):
                if lo < b.hi and b.lo < hi:
                    if is_dma or r.dma or r.eng != op.eng or op.eng != "pe":
                        deps.append(r)
        return deps

    def op(self, eng, fn, reads=(), writes=(), dma=False):
        reads = [Buf(b.key) if b.key.startswith("ps") else b for b in reads]
        writes = [Buf(b.key) if b.key.startswith("ps") else b for b in writes]
        o = _Op()
        o.eng, o.fn, o.dma = eng, fn, dma
        o.signal, o.sigval, o.dsem, o.dval = False, None, None, None
        o.idx = len(self.ops)
        o.deps = self._deps_for(o, reads, writes)
        if dma:
            if eng == "pool":
                base, npool = N_DMA_SEMS - N_SW_SEMS, N_SW_SEMS
                c = self.dma_count_sw
                self.dma_count_sw += 1
            else:
                base, npool = 0, N_DMA_SEMS - N_SW_SEMS
                c = self.dma_count
                self.dma_count += 1
            s = base + c % npool
            u = c // npool
            o.dsem, o.dval = s, 16 * (u + 1)
            prev = self.dma_last.get(s)
            if prev is not None:
                o.deps.append(prev)
            self.dma_last[s] = o
        for b in writes:
            wl = self.writes.setdefault(b.key, [])
            wl[:] = [e for e in wl if not (b.lo <= e[0] and e[1] <= b.hi)]
            wl.append((b.lo, b.hi, o))
            rl = self.reads.get(b.key)
            if rl:
                rl[:] = [e for e in rl if not (b.lo <= e[0] and e[1] <= b.hi)]
        for b in reads:
            rl = self.reads.setdefault(b.key, [])
            if not dma:
                rl[:] = [e for e in rl if not (e[0] == b.lo and e[1] == b.hi
                                               and (not e[2].dma) and e[2].eng == eng)]
            rl.append((b.lo, b.hi, o))
        self.ops.append(o)
        return o

    def emit(self, nc, final_wait_eng="sp"):
        ops = self.ops
        for o in ops:
            for d in o.deps:
                if not d.dma:
                    d.signal = True
        cnt = {e: 0 for e in self.ENGS}
        for o in ops:
            if o.signal:
                cnt[o.eng] += 1
                o.sigval = cnt[o.eng]
        per_eng = {e: [] for e in self.ENGS}
        for o in ops:
            per_eng[o.eng].append(o)
        self.stats = (dict(cnt), self.dma_count, len(ops))
        with ExitStack() as es:
            esem = {e: es.enter_context(nc.semaphore("c_" + e)) for e in self.ENGS}
            dsem = [es.enter_context(nc.semaphore("d_%d" % i)) for i in range(N_DMA_SEMS)]
            block = es.enter_context(nc.Block())

            def run(engname, e):
                waited = {}
                for o in per_eng[engname]:
                    need = {}
                    for d in o.deps:
                        if d.dma:
                            k, v = ("d", d.dsem), d.dval
                        else:
                            k, v = ("e", d.eng), d.sigval
                        if waited.get(k, 0) < v and need.get(k, 0) < v:
                            need[k] = v
                    for k, v in need.items():
                        sem = dsem[k[1]] if k[0] == "d" else esem[k[1]]
                        e.wait_ge(sem, v)
                        waited[k] = v
                    ins = o.fn(e)
                    if o.dma:
                        ins.then_inc(dsem[o.dsem], 16)
                    elif o.signal:
                        ins.then_inc(esem[o.eng], 1)
                if engname == final_wait_eng:
                    for s, last in self.dma_last.items():
                        if waited.get(("d", s), 0) < last.dval:
                            e.wait_ge(dsem[s], last.dval)

            @block.tensor
            def _(e):
                run("pe", e)

            @block.scalar
            def _(e):
                run("act", e)

            @block.vector
            def _(e):
                run("dve", e)

            @block.gpsimd
            def _(e):
                run("pool", e)

            @block.sync
            def _(e):
                run("sp", e)


class T:
    def __init__(self, ap, key, base, esize, n):
        self.ap, self.key, self.base, self.esize, self.n = ap, key, base, esize, n

    def b(self, lo=None, hi=None):
        lo = 0 if lo is None else lo
        hi = self.n if hi is None else hi
        return Buf(self.key, self.base + lo * self.esize, self.base + hi * self.esize)

    def __getitem__(self, k):
        return self.ap[k]


def _core_consts(is_prompt):
    f32 = np.float32
    L, nblk = (256, 4) if is_prompt else (1024, 1)
    c = {}
    t = np.linspace(0.0, 1.0, L, dtype=f32)[:, None]
    w_ang = (2.0 * np.pi * np.arange(L, dtype=f32)[:, None] / L).astype(f32)
    bands = np.linspace(1e-4, 15.0, 16, dtype=f32)[None, :]
    z = np.concatenate([t, np.cos(bands * w_ang), -np.sin(bands * w_ang)], axis=-1).astype(f32)
    c["zembT"] = np.ascontiguousarray(np.tile(z, (nblk, 1)).T)
    deltas = np.abs(np.linspace(np.log(1e-2) / 0.3, np.log(1e-2) / 1.5, 256, dtype=f32))
    win = (np.exp(-t * deltas[None, :]) + 0.05).astype(f32)
    c["window"] = np.ascontiguousarray(np.tile(win, (nblk, 1)))
    m0 = np.ones(1024, f32)
    m0[::L] = 0.0
    c["m0"] = np.ascontiguousarray(m0.reshape(8, 128).T)
    s = np.arange(L, dtype=np.float64)[:, None]
    f = np.arange(L, dtype=np.float64)[None, :]
    ang = np.pi * (f + 0.5) * s / L
    Cb, Sb = np.cos(ang), -np.sin(ang)
    Cf = np.zeros((1024, 1024)); Sf = np.zeros((1024, 1024))
    for b in range(nblk):
        Cf[b * L:(b + 1) * L, b * L:(b + 1) * L] = Cb
        Sf[b * L:(b + 1) * L, b * L:(b + 1) * L] = Sb
    fb = np.stack([Cf, Sf], 0).reshape(2, 8, 128, 8, 128)
    c["fwdB"] = np.ascontiguousarray(fb.transpose(3, 2, 0, 1, 4)).astype(ml_dtypes.bfloat16)
    ib = np.stack([Cf.T / L, Sf.T / L], 0).reshape(2, 8, 128, 1024)
    c["invB"] = np.ascontiguousarray(ib.transpose(1, 2, 0, 3)).astype(ml_dtypes.bfloat16)
    cosf = np.ones((96, 1024), f32); sinf = np.zeros((96, 1024), f32)
    if not is_prompt:
        tok = np.arange(1024)
        row = (tok // 64).astype(f32); col = (tok % 64).astype(f32)
        inv = (1.0 / (10000.0 ** (np.arange(0, 16, 2, dtype=f32) / 16.0))).astype(f32)
        angr = np.concatenate([row[:, None] * inv, col[:, None] * inv], -1).astype(f32)
        cs, sn = np.cos(angr), np.sin(angr)
        for ax in range(2):
            for ab in range(2):
                for i in range(8):
                    d = 64 + ax * 16 + ab * 8 + i
                    cosf[d] = cs[:, ax * 8 + i]
                    sinf[d] = sn[:, ax * 8 + i]
    c["cosf"], c["sinf"] = cosf, sinf
    qm = np.zeros((4, 1024), f32); km = np.zeros((4, NK), f32)
    if is_prompt:
        for e in range(4):
            qm[e, e * 256:(e + 1) * 256] = BIG
            km[e, :] = -1.0
            km[e, e * 256:(e + 1) * 256] = 0.0
    c["qm"] = qm.astype(ml_dtypes.bfloat16)
    c["km"] = km.astype(ml_dtypes.bfloat16)
    c["nb"] = np.full((128, 1), -1.0 if is_prompt else 0.0, f32)
    rot = np.zeros((32, 96), f32); ida = np.zeros((32, 96), f32)
    for ax in range(2):
        for i in range(8):
            a, b = ax * 16 + i, ax * 16 + 8 + i
            rot[b, 64 + a] = -1.0
            rot[a, 64 + b] = 1.0
    for k in range(32):
        ida[k, 64 + k] = 1.0
    c["rot"] = rot.astype(ml_dtypes.bfloat16)
    c["ida"] = ida.astype(ml_dtypes.bfloat16)
    c["ident"] = np.eye(128, dtype=f32)
    return c


W_NAMES = ["w_mod", "b_mod", "g_norm1", "w_in", "hy_conv", "hy_w1", "hy_b1", "hy_freq", "hy_w2", "hy_b2",
           "hy_w3", "hy_bias", "sc_conv", "g_q", "w_uq", "g_kv", "w_ukv", "g_qh", "g_kh", "g_grp", "w_out",
           "g_norm2", "w_ff_gate", "w_ff_up", "w_ff_down"]
W_SHAPES = {"w_mod": [2, 1024, 6144], "b_mod": [2, 6144], "g_norm1": [2, 1024], "w_in": [2, 1024, 1952],
            "hy_conv": [2, 3, 768], "hy_w1": [2, 33, 64], "hy_b1": [2, 64], "hy_freq": [2, 64],
            "hy_w2": [2, 64, 64], "hy_b2": [2, 64], "hy_w3": [2, 64, 512], "hy_bias": [2, 256],
            "sc_conv": [2, 3, 256], "g_q": [2, 256], "w_uq": [2, 256, 768], "g_kv": [2, 128],
            "w_ukv": [2, 128, 1024], "g_qh": [2, 96], "g_kh": [2, 96], "g_grp": [2, 1024],
            "w_out": [2, 1024, 1024], "g_norm2": [2, 1024], "w_ff_gate": [2, 1024, 2816],
            "w_ff_up": [2, 1024, 2816], "w_ff_down": [2, 2816, 1024]}
C_SHAPES = {"zembT": ([33, 1024], F32), "window": ([1024, 256], F32), "m0": ([128, 8], F32),
            "fwdB": ([8, 128, 2, 8, 128], BF16), "invB": ([8, 128, 2, 1024], BF16),
            "cosf": ([96, 1024], F32), "sinf": ([96, 1024], F32), "qm": ([4, 1024], BF16),
            "km": ([4, NK], BF16), "nb": ([128, 1], F32), "rot": ([32, 96], BF16), "ida": ([32, 96], BF16),
            "ident": ([128, 128], F32)}


def build_nc(debug=()):
    nc = bass.Bass("TRN2", target_bir_lowering=False)
    din = {}
    din["x"] = nc.dram_tensor("x", [NT, D], F32, kind="ExternalInput").ap()
    din["cvT"] = nc.dram_tensor("cvT", [128, 8], F32, kind="ExternalInput").ap()
    din["cckv"] = nc.dram_tensor("cckv", [2, 256, 128], F32, kind="ExternalInput").ap()
    din["ckpe"] = nc.dram_tensor("ckpe", [2, 256, 32], F32, kind="ExternalInput").ap()
    for n in W_NAMES:
        din[n] = nc.dram_tensor(n, W_SHAPES[n], F32, kind="ExternalInput").ap()
    for n, (shp, dt) in C_SHAPES.items():
        din[n] = nc.dram_tensor(n, shp, dt, kind="ExternalInput").ap()
    y_out = nc.dram_tensor("y", [NT, D], F32, kind="ExternalOutput").ap()
    nckv_out = nc.dram_tensor("nckv", [2, NT, 128], F32, kind="ExternalOutput").ap()
    nkpe_out = nc.dram_tensor("nkpe", [2, NT, 32], F32, kind="ExternalOutput").ap()
    dbg_out = {}

    P = Prog()
    es = ExitStack()
    with es:
        uid = [0]

        def sb(shape, dt, name=None):
            uid[0] += 1
            name = "s_" + (name or ("t%d" % uid[0]))
            h = es.enter_context(nc.sbuf_tensor(name, shape, dt))
            n = int(np.prod(shape[1:]))
            return T(h.ap(), name, 0, 2 if dt == BF16 else 4, n)

        ARENA_F32 = 12800
        arena_h = es.enter_context(nc.sbuf_tensor("arena", [128, ARENA_F32], F32))
        g_h = [es.enter_context(nc.sbuf_tensor("G%d" % i, [128, 2048], F32)) for i in range(3)]

        class Carver:
            def __init__(self, handle=None, key="arena", size=None):
                self.off = 0
                self.h = arena_h if handle is None else handle
                self.key = key
                self.size = ARENA_F32 * 4 if size is None else size

            def get(self, shape, dt):
                es_ = 2 if dt == BF16 else 4
                n = int(np.prod(shape[1:]))
                nb = (n * es_ + 3) // 4 * 4
                lo = self.off
                self.off += nb
                assert self.off <= self.size, "arena overflow %s %d" % (self.key, self.off)
                ap = self.h[:, lo // 4:(lo + nb) // 4]
                if dt == BF16:
                    ap = ap.bitcast(BF16)
                    if n * 2 < nb:
                        ap = ap[:, 0:n]
                if len(shape) == 3:
                    ap = ap.rearrange("p (a b) -> p a b", a=shape[1])
                if shape[0] < 128:
                    ap = ap[0:shape[0]]
                return T(ap, self.key, lo, es_, n)

        psb = []
        for i in range(8):
            h = es.enter_context(nc.psum_tensor("ps%d" % i, [128, 512], F32))
            psb.append(T(h[:], "ps%d" % i, 0, 4, 512))
        psi = [0]
        rrmod = [5]

        def nps():
            if pool_lo[0] is None:
                p = psb[psi[0] % rrmod[0]]
                psi[0] += 1
            else:
                p = psb[pool_lo[0] + psi2[0] % pool_n[0]]
                psi2[0] += 1
            return p
        pool_lo, pool_n, psi2 = [None], [0], [0]
        psi3 = [0]

        def nps_s():
            p = psb[psi3[0] % 3]
            psi3[0] += 1
            return p

        def dma(eng, out, in_, reads=(), writes=(), **kw):
            return P.op(eng, lambda e: e.dma_start(out=out, in_=in_, **kw), reads=reads, writes=writes, dma=True)

        def mm(out, lhsT, rhs, start, stop, reads, writes):
            return P.op("pe", lambda e: e.matmul(out, lhsT=lhsT, rhs=rhs, start=start, stop=stop),
                        reads=reads, writes=writes)

        def tr(out, in_, ident, reads, writes):
            return P.op("pe", lambda e: e.transpose(out, in_, ident), reads=reads, writes=writes)

        def act(out, in_, func, reads, writes, scale=1.0, bias=0.0):
            return P.op("act", lambda e: e.activation(out=out, in_=in_, func=func, bias=bias, scale=scale),
                        reads=reads, writes=writes)

        def tt(eng, out, a, b, op, reads, writes):
            return P.op(eng, lambda e: e.tensor_tensor(out=out, in0=a, in1=b, op=op), reads=reads, writes=writes)

        def ts(out, a, s1, s2, op0, op1, reads, writes):
            return P.op("dve", lambda e: e.tensor_scalar(out, a, s1, s2, op0, op1), reads=reads, writes=writes)

        def tsm(out, a, s1, reads, writes):
            return P.op("dve", lambda e: e.tensor_scalar_mul(out, a, s1), reads=reads, writes=writes)

        def stt(out, a, s, b, op0, op1, reads, writes):
            return P.op("dve", lambda e: e.scalar_tensor_tensor(out=out, in0=a, scalar=s, in1=b, op0=op0, op1=op1),
                        reads=reads, writes=writes)

        def cp(eng, out, in_, reads, writes):
            if eng == "act":
                return act(out, in_, AF.Copy, reads, writes)
            return P.op(eng, lambda e: e.tensor_copy(out=out, in_=in_), reads=reads, writes=writes)

        def recip(out, in_, reads, writes):
            return P.op("dve", lambda e: e.reciprocal(out, in_), reads=reads, writes=writes)

        def memset(eng, t_, val, ap=None):
            a = t_.ap if ap is None else ap
            return P.op(eng, lambda e: e.memset(a, val), writes=[t_.b()])

        def dbg(name, t_, shape):
            if name in debug:
                o = nc.dram_tensor("dbg_" + name, shape, F32 if t_.esize == 4 else BF16, kind="ExternalOutput").ap()
                dbg_out[name] = o
                dma("sp", o, t_.ap, reads=[t_.b()])

        xT = sb([128, 8, NT], F32, "xT")
        hT = sb([128, 8, NT], BF16, "hT")
        yT = sb([128, 8, NT], BF16, "yT")
        wring = [sb([128, 8192], BF16, "wr0"), sb([128, 8192], BF16, "wr1")]
        wri = [0]

        def wslot():
            s_ = wring[wri[0] % 2]
            wri[0] += 1
            return s_

        ident = sb([128, 128], F32, "ident")
        ident16 = sb([128, 128], BF16, "ident16")
        ones16 = sb([128, 128], BF16, "ones16")
        epsc = sb([128, 1], F32, "epsc")
        lnsc = sb([128, 1], F32, "lnsc")
        nbc = sb([128, 1], F32, "nbc")
        cosf = sb([96, NT], F32, "cosf")
        sinf = sb([96, NT], F32, "sinf")
        vstage = [sb([128, 128], F32, "vst%d" % l) for l in range(DEPTH)]
        vecT = [sb([128, 128], F32, "vecT%d" % l) for l in range(DEPTH)]
        modT = [sb([128, 48], F32, "modT%d" % l) for l in range(DEPTH)]
        gm = [sb([128, 16], F32, "gm%d" % l) for l in range(DEPTH)]
        nbw = [sb([128, 24], F32, "nbw%d" % l) for l in range(DEPTH)]
        frb = [sb([64, 8], F32, "frb%d" % l) for l in range(DEPTH)]
        scv16 = sb([128, 8], BF16, "scv16")
        cv = sb([128, 8], F32, "cv")
        rstd = [sb([128, 512], F32, "rstd%d" % i) for i in range(2)]
        tmpf = [sb([128, 512], F32, "tmpf%d" % i) for i in range(3)]
        sq16 = [sb([128, 512], BF16, "sq16_%d" % i) for i in range(3)]
        cnt = {"rstd": 0, "tmpf": 0, "sq": 0}

        def ring(name, lst):
            t_ = lst[cnt[name] % len(lst)]
            cnt[name] += 1
            return t_

        G = [T(g_h[i].ap().rearrange("p (a b) -> p a b", a=2), "G%d" % i, 0, 4, 2048) for i in range(3)]
        upad = sb([128, NT + 2], F32, "upad")
        stage = sb([128, 1024], F32, "stage")
        convo = stage

        dma("sp", ident.ap, din["ident"], writes=[ident.b()])
        cp("dve", ident16.ap, ident.ap, [ident.b()], [ident16.b()])
        memset("dve", ones16, 1.0)
        memset("dve", epsc, EPS)
        memset("dve", lnsc, float(np.log(96.0 ** -0.5)))
        memset("dve", upad, 0.0)
        dma("sp", nbc.ap, din["nb"], writes=[nbc.b()])
        dma("sp", cosf.ap, din["cosf"], writes=[cosf.b()])
        dma("sp", sinf.ap, din["sinf"], writes=[sinf.b()])
        dma("sp", cv.ap, din["cvT"], writes=[cv.b()])

        VOFF = {}
        r = 0
        for name, nrows in [("b_mod", 48), ("g_norm1", 8), ("g_norm2", 8), ("g_grp", 8), ("hy_conv", 18),
                            ("sc_conv", 6), ("hy_bias", 2), ("g_q", 2), ("g_kv", 1), ("g_qh", 1), ("g_kh", 1),
                            ("gsw_q", 1), ("gsw_k", 1), ("hy_b1", 1), ("hy_freq", 1), ("hy_b2", 1)]:
            VOFF[name] = r
            r += nrows
        assert r <= 128
        for l in range(DEPTH):
            vs = vstage[l]
            pend = []

            def vrow(name, src, nrows, n, c0=0, roff=0):
                r0 = VOFF[name] + roff
                pend.append((vs[r0:r0 + nrows, c0:c0 + n], src))
            vrow("b_mod", din["b_mod"][l].rearrange("(r p) -> r p", p=128), 48, 128)
            vrow("g_norm1", din["g_norm1"][l].rearrange("(r p) -> r p", p=128), 8, 128)
            vrow("g_norm2", din["g_norm2"][l].rearrange("(r p) -> r p", p=128), 8, 128)
            vrow("g_grp", din["g_grp"][l].rearrange("(r p) -> r p", p=128), 8, 128)
            for k in range(3):
                vrow("hy_conv", din["hy_conv"][l, k].rearrange("(c p) -> c p", p=128), 6, 128, roff=6 * k)
                vrow("sc_conv", din["sc_conv"][l, k].rearrange("(c p) -> c p", p=128), 2, 128, roff=2 * k)
            vrow("hy_bias", din["hy_bias"][l].rearrange("(r p) -> r p", p=128), 2, 128)
            vrow("g_q", din["g_q"][l].rearrange("(r p) -> r p", p=128), 2, 128)
            vrow("g_kv", din["g_kv"][l:l + 1, :], 1, 128)
            vrow("g_qh", din["g_qh"][l:l + 1, :], 1, 96)
            vrow("g_kh", din["g_kh"][l:l + 1, :], 1, 96)
            for nm, src in (("gsw_q", "g_qh"), ("gsw_k", "g_kh")):
                for ax in range(2):
                    a0 = 64 + ax * 16
                    vrow(nm, din[src][l:l + 1, a0 + 8:a0 + 16], 1, 8, c0=a0)
                    vrow(nm, din[src][l:l + 1, a0:a0 + 8], 1, 8, c0=a0 + 8)
            vrow("hy_b1", din["hy_b1"][l:l + 1, :], 1, 64)
            vrow("hy_freq", din["hy_freq"][l:l + 1, :], 1, 64)
            vrow("hy_b2", din["hy_b2"][l:l + 1, :], 1, 64)
            vkeys = [Buf("vsk%d_%d" % (l, i)) for i in range(len(pend))]
            P.op("dve", lambda e, a=vs.ap: e.memset(a, 0.0), writes=[vs.b()] + vkeys)
            for (dst, src), kb in zip(pend, vkeys):
                dma("sp", dst, src, writes=[kb])
            vs_all = [vs.b()] + vkeys
            p = nps()
            tr(p[:, 0:128], vs.ap, ident.ap, vs_all + [ident.b()], [p.b(0, 128)])
            cp("dve", vecT[l].ap, p[:, 0:128], [p.b(0, 128)], [vecT[l].b()])
            c0 = VOFF["hy_conv"]
            tsm(nbw[l].ap, vecT[l][:, c0:c0 + 24], nbc[:, 0:1], [vecT[l].b(), nbc.b()], [nbw[l].b()])
            vT = vecT[l]
            cf, cb1, cb2 = VOFF["hy_freq"], VOFF["hy_b1"], VOFF["hy_b2"]
            tsm(frb[l][:, 0:1], vT[0:64, cf:cf + 1], 0.25, [vT.b()], [frb[l].b()])
            tt("dve", frb[l][:, 1:2], frb[l][:, 0:1], vT[0:64, cb1:cb1 + 1], ALU.mult, [vT.b(), frb[l].b()], [frb[l].b()])
            tt("dve", frb[l][:, 2:3], frb[l][:, 0:1], vT[0:64, cb2:cb2 + 1], ALU.mult, [vT.b(), frb[l].b()], [frb[l].b()])
            tsm(frb[l][:, 3:6], frb[l][:, 0:3], 0.5, [frb[l].b()], [frb[l].b()])

        def vcol(l, name, j=0, n=128):
            c0 = VOFF[name] + j
            return vecT[l][0:n, c0:c0 + 1]

        act(cv.ap, cv.ap, AF.Silu, [cv.b()], [cv.b()])
        cp("dve", scv16.ap, cv.ap, [cv.b()], [scv16.b()])

        stage2 = T(g_h[0].ap()[:, 0:1024], "G0", 0, 4, 1024)

        def load_x(tc):
            st_ = stage if tc % 2 == 0 else stage2
            dma("sp", st_.ap, din["x"][tc * 128:(tc + 1) * 128, :], writes=[st_.b()])
            for hh in range(2):
                p = nps()
                for q in range(4):
                    dc = hh * 4 + q
                    tr(p[:, q * 128:(q + 1) * 128], st_[:, dc * 128:(dc + 1) * 128], ident.ap,
                       [st_.b(), ident.b()], [p.b(q * 128, (q + 1) * 128)])
                cp("act" if hh else "dve", xT[:, hh * 4:hh * 4 + 4, tc * 128:(tc + 1) * 128],
                   p.ap.rearrange("p (a b) -> p a b", a=4), [p.b()],
                   [xT.b(dc_ * NT + tc * 128, dc_ * NT + (tc + 1) * 128) for dc_ in range(hh * 4, hh * 4 + 4)])

        def step(g):
            if g is not None:
                try:
                    next(g)
                except StopIteration:
                    return None
            return g

        def drain(g):
            while g is not None:
                g = step(g)

        def mod_gen(l, holder):
            pm = psb[5]
            for pc in range(12):
                slot = holder[0]
                halves = [(slot.ap[:, 0:4096].rearrange("p (a b) -> p a b", a=8), slot.b(0, 4096)),
                          (slot.ap[:, 4096:8192].rearrange("p (a b) -> p a b", a=8), slot.b(4096, 8192))]
                wv, wb = halves[pc % 2]
                dma("pool", wv, din["w_mod"][l][:, pc * 512:(pc + 1) * 512].rearrange("(kc p) n -> p kc n", p=128),
                    writes=[wb])
                for jj in range(4):
                    j = pc * 4 + jj
                    for kc in range(8):
                        mm(pm[:, 16 + j:17 + j], wv[:, kc, jj * 128:(jj + 1) * 128], scv16[:, kc:kc + 1],
                           kc == 0, kc == 7, [wb, scv16.b()], [pm.b()])
                    yield
                c0 = VOFF["b_mod"] + 4 * pc
                tt("dve", modT[l][:, 4 * pc:4 * pc + 4], pm[:, 16 + 4 * pc:20 + 4 * pc], vecT[l][:, c0:c0 + 4], ALU.add,
                   [pm.b(), vecT[l].b()], [modT[l].b(4 * pc, 4 * pc + 4)])
                for i, (nm, sc0, pcd) in enumerate((("g_norm1", 8, 3), ("g_norm2", 32, 9))):
                    if pc == pcd:
                        g0 = VOFF[nm]
                        stt(gm[l][:, 8 * i:8 * i + 8], modT[l][:, sc0:sc0 + 8], 1.0, vecT[l][:, g0:g0 + 8], ALU.add, ALU.mult,
                            [modT[l].b(sc0, sc0 + 8), vecT[l].b()], [gm[l].b(8 * i, 8 * i + 8)])
                yield

        def rstd_from_psum(pn, nparts, nfeat, out_t):
            act(out_t[0:nparts, :], pn[0:nparts, :], AF.Ln, [pn.b(), epsc.b()], [out_t.b()],
                scale=1.0 / nfeat, bias=epsc[0:nparts, 0:1])
            act(out_t[0:nparts, :], out_t[0:nparts, :], AF.Exp, [out_t.b()], [out_t.b()], scale=-0.5)

        def norm_mod(l, which):
            gofs = 8 * which
            shofs = 0 if which == 0 else 24
            for (t0, t1) in TT:
                pn = nps()
                for dc in range(8):
                    s_ = ring("sq", sq16)
                    act(s_.ap, xT[:, dc, t0:t1], AF.Square, [xT.b(dc * NT + t0, dc * NT + t1)], [s_.b()])
                    mm(pn.ap, ones16.ap, s_.ap, dc == 0, dc == 7, [ones16.b(), s_.b()], [pn.b()])
                rs = ring("rstd", rstd)
                rstd_from_psum(pn, 128, D, rs)
                for dc in range(8):
                    tf = ring("tmpf", tmpf)
                    stt(tf.ap, xT[:, dc, t0:t1], gm[l][:, gofs + dc:gofs + dc + 1], rs.ap, ALU.mult, ALU.mult,
                        [xT.b(dc * NT + t0, dc * NT + t1), gm[l].b(gofs, gofs + 8), rs.b()], [tf.b()])
                    act(hT[:, dc, t0:t1], tf.ap, AF.Identity, [tf.b(), modT[l].b(shofs, shofs + 8)],
                        [hT.b(dc * NT + t0, dc * NT + t1)], bias=modT[l][:, shofs + dc:shofs + dc + 1])

        def proj_chunk(ws, wv, col0, m, t0, t1):
            p = nps()
            for kc in range(8):
                mm(p[0:m, 0:t1 - t0], wv[:, kc, col0:col0 + m], hT[:, kc, t0:t1], kc == 0, kc == 7,
                   [ws.b(), hT.b(kc * NT + t0, kc * NT + t1)], [p.b()])
            return p

        def load_win_piece(l, c0, ncols):
            ws = wslot()
            wv = ws.ap[:, 0:8 * ncols].rearrange("p (a b) -> p a b", a=8)
            dma("pool", wv, din["w_in"][l][:, c0:c0 + ncols].rearrange("(kc p) n -> p kc n", p=128), writes=[ws.b()])
            return ws, wv

        def conv3(l, wname, widx, nbidx, dst_ap, dst_bufs):
            w = [vcol(l, wname, widx[k]) for k in range(3)]
            rb = [upad.b(), vecT[l].b()]
            tsm(dst_ap, upad[:, 1:NT + 1], w[1], rb, dst_bufs)
            stt(dst_ap, upad[:, 0:NT], w[0], dst_ap, ALU.mult, ALU.add, rb + dst_bufs, dst_bufs)
            stt(dst_ap, upad[:, 2:NT + 2], w[2], dst_ap, ALU.mult, ALU.add, rb + dst_bufs, dst_bufs)
            n0 = nbw[l][:, nbidx[0]:nbidx[0] + 1]
            n2 = nbw[l][:, nbidx[2]:nbidx[2] + 1]
            rb2 = [upad.b(), nbw[l].b()]
            stt(dst_ap[:, 256:1024:256], upad[:, 256:1024:256], n0, dst_ap[:, 256:1024:256], ALU.mult, ALU.add,
                rb2 + dst_bufs, dst_bufs)
            stt(dst_ap[:, 255:1023:256], upad[:, 257:1025:256], n2, dst_ap[:, 255:1023:256], ALU.mult, ALU.add,
                rb2 + dst_bufs, dst_bufs)

        def sin_mlp(pre, nparts, l, bcol, out_t, t0, t1, car):
            s4, s8, c4 = car["s4"], car["s8"], car["c4"]
            n = t1 - t0
            act(s4[0:nparts, 0:n], pre, AF.Sin, [car["pre"].b(), frb[l].b()], [s4.b()],
                scale=frb[l][0:nparts, 0:1], bias=frb[l][0:nparts, bcol:bcol + 1])
            act(s8[0:nparts, 0:n], pre, AF.Sin, [car["pre"].b(), frb[l].b()], [s8.b()],
                scale=frb[l][0:nparts, 3:4], bias=frb[l][0:nparts, bcol + 3:bcol + 4])
            tt("dve", c4[0:nparts, 0:n], s8[0:nparts, 0:n], s8[0:nparts, 0:n], ALU.mult, [s8.b()], [c4.b()])
            ts(c4[0:nparts, 0:n], c4[0:nparts, 0:n], -2.0, 1.0, ALU.mult, ALU.add, [c4.b()], [c4.b()])
            stt(s8[0:nparts, 0:n], s4[0:nparts, 0:n], 2.0, c4[0:nparts, 0:n], ALU.mult, ALU.mult, [s4.b(), c4.b()], [s8.b()])
            tt("dve", c4[0:nparts, 0:n], s4[0:nparts, 0:n], s4[0:nparts, 0:n], ALU.mult, [s4.b()], [c4.b()])
            ts(c4[0:nparts, 0:n], c4[0:nparts, 0:n], -2.0, 1.0, ALU.mult, ALU.add, [c4.b()], [c4.b()])
            stt(out_t[0:nparts, t0:t1], s8[0:nparts, 0:n], 2.0, c4[0:nparts, 0:n], ALU.mult, ALU.mult,
                [s8.b(), c4.b()], [out_t.b(t0, t1)])

        class _Stop(Exception):
            pass

        def stop(name):
            if ("stop:" + name) in debug:
                raise _Stop()

        def layer(l):
            mT = modT[l]
            stop("s0")
            norm_mod(l, 0)
            dbg("hT%d" % l, hT, [128, 8, NT])
            stop("norm")

            car = Carver()
            x0T, x1T, zT = G[0], G[1], G[2]
            z16T = car.get([128, 2, NT], BF16)
            ztok = car.get([128, 8, 256], BF16)
            hp = car.get([128, 8, 256], BF16)
            hm = car.get([128, 8, 256], BF16)
            Yt = car.get([128, 8, 512], BF16)
            t1f = car.get([128, 256], F32)
            t2f = car.get([128, 256], F32)
            mark = car.off
            h1T = car.get([64, NT], F32)
            h2T = car.get([64, NT], BF16)
            s4 = car.get([64, 512], F32)
            s8 = car.get([64, 512], F32)
            c4 = car.get([64, 512], F32)
            wnd = [car.get([128, 256], F32) for _ in range(2)]
            w1s = car.get([33, 64], F32)
            w2s = car.get([64, 64], F32)
            w3s = car.get([64, 512], BF16)
            zemb = car.get([33, NT], F32)
            car.off = mark
            dring = [car.get([128, 2, 1024], BF16) for _ in range(2)]
            Ksb = car.get([128, 512], F32)

            def filt_gen():
                dma("sp", w1s.ap, din["hy_w1"][l], writes=[w1s.b()])
                dma("sp", w2s.ap, din["hy_w2"][l], writes=[w2s.b()])
                dma("pool", w3s.ap, din["hy_w3"][l], writes=[w3s.b()])
                dma("sp", zemb.ap, din["zembT"], writes=[zemb.b()])
                carry = {"s4": s4, "s8": s8, "c4": c4}
                for (t0, t1) in TT:
                    p = nps()
                    mm(p[0:64, :], w1s.ap, zemb[:, t0:t1], True, True, [w1s.b(), zemb.b()], [p.b()])
                    carry["pre"] = p
                    sin_mlp(p[0:64, :], 64, l, 1, h1T, t0, t1, carry)
                    yield
                for (t0, t1) in TT:
                    p = nps()
                    mm(p[0:64, :], w2s.ap, h1T[:, t0:t1], True, True, [w2s.b(), h1T.b(t0, t1)], [p.b()])
                    carry["pre"] = p
                    sin_mlp(p[0:64, :], 64, l, 2, h2T, t0, t1, carry)
                    yield
                for tcx in range(8):
                    wn = wnd[tcx % 2]
                    dma("sp", wn.ap, din["window"][tcx * 128:(tcx + 1) * 128, :], writes=[wn.b()])
                    p = nps()
                    mm(p.ap, h2T[:, tcx * 128:(tcx + 1) * 128], w3s.ap, True, True, [h2T.b(), w3s.b()], [p.b()])
                    tt("dve", t1f.ap, p[:, 0:256], wn.ap, ALU.mult, [p.b(), wn.b()], [t1f.b()])
                    stt(t2f.ap, p[:, 256:512], m0c[:, tcx:tcx + 1], wn.ap, ALU.mult, ALU.mult, [p.b(), wn.b(), m0c.b()], [t2f.b()])
                    tt("dve", hp[:, tcx, :], t1f.ap, t2f.ap, ALU.add, [t1f.b(), t2f.b()], [hp.b(tcx * 256, (tcx + 1) * 256)])
                    tt("dve", hm[:, tcx, :], t1f.ap, t2f.ap, ALU.subtract, [t1f.b(), t2f.b()], [hm.b(tcx * 256, (tcx + 1) * 256)])
                    yield
            ws, wv = load_win_piece(l, 0, 768)
            gfilt = filt_gen()
            for c in range(6):
                for (t0, t1) in TT:
                    p = proj_chunk(ws, wv, c * 128, 128, t0, t1)
                    cp("act", upad[:, 1 + t0:1 + t1], p.ap, [p.b()], [upad.b()])
                widx = [k * 6 + c for k in range(3)]
                if c < 2:
                    conv3(l, "hy_conv", widx, widx, x0T[:, c, :], [x0T.b(c * NT, (c + 1) * NT)])
                elif c < 4:
                    conv3(l, "hy_conv", widx, widx, x1T[:, c - 2, :], [x1T.b((c - 2) * NT, (c - 1) * NT)])
                else:
                    cc = c - 4
                    conv3(l, "hy_conv", widx, widx, convo.ap, [convo.b()])
                    tt("dve", zT[:, cc, :], x1T[:, cc, :], convo.ap, ALU.mult,
                       [x1T.b(cc * NT, (cc + 1) * NT), convo.b()], [zT.b(cc * NT, (cc + 1) * NT)])
                    cp("act", z16T[:, cc, :], zT[:, cc, :], [zT.b(cc * NT, (cc + 1) * NT)], [z16T.b(cc * NT, (cc + 1) * NT)])
                for _ in range(2):
                    gfilt = step(gfilt)
                if l == 0:
                    gmod0[0] = step(gmod0[0])
            stop("hyconv")
            for hh in range(2):
                p = nps()
                pb = p.ap.bitcast(BF16)
                for q in range(4):
                    tcx = hh * 4 + q
                    for cc in range(2):
                        o0 = q * 256 + cc * 128
                        tr(pb[:, o0:o0 + 128], z16T[:, cc, tcx * 128:(tcx + 1) * 128], ident16.ap,
                           [z16T.b(), ident16.b()], [p.b()])
                cp("dve", ztok[:, hh * 4:hh * 4 + 4, :], pb.rearrange("p (a b) -> p a b", a=4), [p.b()], [ztok.b()])
            stop("ztr")
            drain(gfilt)
            stop("filt")
            for j in range(8):
                dr = dring[j % 2]
                drv = dr.ap.rearrange("p c (k m) -> p c k m", k=8)
                dma("sp", dr.ap, din["fwdB"][j].rearrange("p c k m -> p c (k m)"), writes=[dr.b()])
                pk, pz = nps(), nps()
                for half, (mat, rhs_t) in enumerate(((0, hp), (1, hm))):
                    for kc in range(8):
                        mm(pk[:, half * 256:(half + 1) * 256], drv[:, mat, kc, :], rhs_t[:, kc, :], kc == 0, kc == 7,
                           [dr.b(), rhs_t.b()], [pk.b(half * 256, (half + 1) * 256)])
                for half in range(2):
                    for kc in range(8):
                        mm(pz[:, half * 256:(half + 1) * 256], drv[:, half, kc, :], ztok[:, kc, :], kc == 0, kc == 7,
                           [dr.b(), ztok.b()], [pz.b(half * 256, (half + 1) * 256)])
                cp("act", Ksb.ap, pk.ap, [pk.b()], [Ksb.b()])
                tt("dve", t1f.ap, pz[:, 0:256], Ksb[:, 0:256], ALU.mult, [pz.b(), Ksb.b()], [t1f.b()])
                tt("dve", t2f.ap, pz[:, 256:512], Ksb[:, 256:512], ALU.mult, [pz.b(), Ksb.b()], [t2f.b()])
                tt("dve", Yt[:, j, 0:256], t1f.ap, t2f.ap, ALU.subtract, [t1f.b(), t2f.b()], [Yt.b(j * 512, j * 512 + 256)])
                tt("dve", t1f.ap, pz[:, 0:256], Ksb[:, 256:512], ALU.mult, [pz.b(), Ksb.b()], [t1f.b()])
                tt("dve", t2f.ap, pz[:, 256:512], Ksb[:, 0:256], ALU.mult, [pz.b(), Ksb.b()], [t2f.b()])
                tt("dve", Yt[:, j, 256:512], t1f.ap, t2f.ap, ALU.add, [t1f.b(), t2f.b()], [Yt.b(j * 512 + 256, (j + 1) * 512)])
            stop("spec")
            acc = [[psb[4 + 2 * a_ + b_] for b_ in range(2)] for a_ in range(2)]
            for j in range(8):
                dr = dring[j % 2]
                dma("sp", dr.ap, din["invB"][j], writes=[dr.b()])
                for cc in range(2):
                    for ti, (t0, t1) in enumerate(TT):
                        pa = acc[cc][ti]
                        mm(pa.ap, Yt[:, j, cc * 128:(cc + 1) * 128], dr[:, 0, t0:t1], j == 0, False,
                           [Yt.b(j * 512, (j + 1) * 512), dr.b()], [pa.b()])
                        mm(pa.ap, Yt[:, j, 256 + cc * 128:256 + (cc + 1) * 128], dr[:, 1, t0:t1], False, j == 7,
                           [Yt.b(j * 512, (j + 1) * 512), dr.b()], [pa.b()])
            for cc in range(2):
                for ti, (t0, t1) in enumerate(TT):
                    pa = acc[cc][ti]
                    tf = ring("tmpf", tmpf)
                    stt(tf.ap, zT[:, cc, t0:t1], vcol(l, "hy_bias", cc), pa.ap, ALU.mult, ALU.add,
                        [zT.b(cc * NT + t0, cc * NT + t1), vecT[l].b(), pa.b()], [tf.b()])
                    tt("dve", yT[:, cc, t0:t1], tf.ap, x0T[:, cc, t0:t1], ALU.mult,
                       [tf.b(), x0T.b(cc * NT + t0, cc * NT + t1)], [yT.b(cc * NT + t0, cc * NT + t1)])
            dbg("yhy%d" % l, yT, [128, 8, NT])
            stop("hy")

            bgT, cgT = G[1], G[2]
            ws, wv = load_win_piece(l, 768, 768)
            for c in range(6):
                for (t0, t1) in TT:
                    p = proj_chunk(ws, wv, c * 128, 128, t0, t1)
                    if c < 2:
                        cp("act", bgT[:, c, t0:t1], p.ap, [p.b()], [bgT.b(c * NT + t0, c * NT + t1)])
                    elif c < 4:
                        cp("act", cgT[:, c - 2, t0:t1], p.ap, [p.b()], [cgT.b((c - 2) * NT + t0, (c - 2) * NT + t1)])
                    else:
                        cc = c - 4
                        tt("dve", upad[:, 1 + t0:1 + t1], p.ap, cgT[:, cc, t0:t1], ALU.mult,
                           [p.b(), cgT.b(cc * NT + t0, cc * NT + t1)], [upad.b()])
                if c >= 4:
                    cc = c - 4
                    widx = [k * 2 + cc for k in range(3)]
                    conv3(l, "sc_conv", widx, [18 + w_ for w_ in widx], convo.ap, [convo.b()])
                    tt("dve", yT[:, 2 + cc, :], convo.ap, bgT[:, cc, :], ALU.mult,
                       [convo.b(), bgT.b(cc * NT, (cc + 1) * NT)], [yT.b((2 + cc) * NT, (3 + cc) * NT)])
            dbg("ysc%d" % l, yT, [128, 8, NT])
            stop("sc")

            car = Carver()
            cqT = G[0]
            car1 = Carver(g_h[1], "G1", 8192)
            car2 = Carver(g_h[2], "G2", 8192)
            ckvT = car1.get([128, NK], F32)
            esc = car1.get([128, 80], F32)
            rot16 = car1.get([32, 96], BF16)
            ida16 = car1.get([32, 96], BF16)
            cqn16 = car2.get([128, 2, NT], BF16)
            wq16 = car2.get([128, 2, 768], BF16)
            ctxs = car.get([128, 2, 160], F32)
            kpeT = car.get([32, NK], F32)
            ckvn16 = car.get([128, NK], BF16)
            kpe16 = car.get([32, NK], BF16)
            V16 = car.get([128, 10, 768], BF16)
            wqr16 = car.get([128, 2, 768], BF16)
            wka16 = car.get([128, 8, 96], BF16)
            wv16 = car.get([128, 8, 64], BF16)
            SRk = T(stage.ap[0:96, :], stage.key, 0, 4, 1024)
            qT16 = [car.get([100, NT], BF16) for _ in range(2)]
            kT16 = [car.get([100, NK], BF16) for _ in range(2)]
            PT16 = [car.get([128, 512], BF16) for _ in range(5)]
            rD = [car.get([128, 512], F32) for _ in range(2)]

            dma("sp", ctxs[:, :, 0:128], din["cckv"][l].rearrange("(a p) n -> p a n", p=128), writes=[ctxs.b()])
            dma("sp", ctxs[:, :, 128:160], din["ckpe"][l].rearrange("(a p) n -> p a n", p=128), writes=[ctxs.b()])
            ws, wv = load_win_piece(l, 1536, 416)
            wso = wslot()
            wov = wso.ap.rearrange("p (a b) -> p a b", a=8)
            dma("pool", wov, din["w_out"][l].rearrange("(kc p) n -> p kc n", p=128), writes=[wso.b()])
            dma("sp", rot16.ap, din["rot"], writes=[rot16.b()])
            dma("sp", ida16.ap, din["ida"], writes=[ida16.b()])
            dma("pool", wq16.ap, din["w_uq"][l].rearrange("(kc p) n -> p kc n", p=128), writes=[wq16.b()])
            v1 = V16.ap.rearrange("p k (j t d) -> p k j t d", j=4, t=3, d=64)[:, :, :, 1, :]
            P.op("pool", lambda e: e.memset(v1, 1.0), writes=[V16.b()])
            memset("pool", wka16, 0.0)
            wsrc = din["w_ukv"][l].rearrange("k (h t d) -> k h t d", h=8, t=2, d=64)
            dma("pool", wka16[:, :, 0:64], wsrc[:, :, 0, :], writes=[wka16.b()])
            dma("pool", wv16.ap, wsrc[:, :, 1, :], writes=[wv16.b()])
            memset("pool", wqr16, 0.0)
            for kc in range(2):
                src = wq16[:, kc, :].rearrange("p (h g a i) -> p h g a i", h=8, g=6, a=2, i=8)
                dst = wqr16[:, kc, :].rearrange("p (h g a i) -> p h g a i", h=8, g=6, a=2, i=8)
                tsm(dst[:, :, 4:6, 0, :], src[:, :, 4:6, 1, :], -1.0, [wq16.b()], [wqr16.b()])
                cp("dve", dst[:, :, 4:6, 1, :], src[:, :, 4:6, 0, :], [wq16.b()], [wqr16.b()])
            for i in range(2):
                dma("sp", qT16[i][96:100, :], din["qm"], writes=[qT16[i].b()])
                dma("sp", kT16[i][96:100, :], din["km"], writes=[kT16[i].b()])
            for (t0, t1) in TT:
                for c in range(2):
                    p = proj_chunk(ws, wv, c * 128, 128, t0, t1)
                    cp("act", cqT[:, c, t0:t1], p.ap, [p.b()], [cqT.b(c * NT + t0, c * NT + t1)])
                p = proj_chunk(ws, wv, 256, 128, t0, t1)
                cp("act", ckvT[:, t0:t1], p.ap, [p.b()], [ckvT.b(t0, t1)])
                p = proj_chunk(ws, wv, 384, 32, t0, t1)
                cp("act", kpeT[:, t0:t1], p[0:32, :], [p.b()], [kpeT.b(t0, t1)])
            for hh in range(2):
                p = nps()
                p2 = nps()
                for q in range(4):
                    tcx = hh * 4 + q
                    tr(p[:, q * 128:(q + 1) * 128], ckvT[:, tcx * 128:(tcx + 1) * 128], ident.ap,
                       [ckvT.b(0, NT), ident.b()], [p.b(q * 128, (q + 1) * 128)])
                    tr(p2[:, q * 32:(q + 1) * 32], kpeT[:, tcx * 128:(tcx + 1) * 128], ident[0:32, 0:32],
                       [kpeT.b(0, NT), ident.b()], [p2.b(q * 32, (q + 1) * 32)])
                cp("dve", stage[:, 0:512], p.ap, [p.b()], [stage.b()])
                cp("dve", stage[:, 512:640], p2[:, 0:128], [p2.b()], [stage.b()])
                dma("sp", nckv_out[l].rearrange("(tc p) n -> p tc n", p=128)[:, hh * 4:hh * 4 + 4, :],
                    stage[:, 0:512].rearrange("p (a b) -> p a b", a=4), reads=[stage.b()])
                dma("sp", nkpe_out[l].rearrange("(tc p) n -> p tc n", p=128)[:, hh * 4:hh * 4 + 4, :],
                    stage[:, 512:640].rearrange("p (a b) -> p a b", a=4), reads=[stage.b()])
            p = nps()
            for a in range(2):
                tr(p[:, a * 128:(a + 1) * 128], ctxs[:, a, 0:128], ident.ap, [ctxs.b(), ident.b()], [p.b(a * 128, (a + 1) * 128)])
                tr(p[0:32, 256 + a * 128:256 + (a + 1) * 128], ctxs[:, a, 128:160], ident.ap, [ctxs.b(), ident.b()],
                   [p.b(256 + a * 128, 256 + (a + 1) * 128)])
            cp("dve", ckvT[:, NT:NK], p[:, 0:256], [p.b()], [ckvT.b(NT, NK)])
            cp("dve", kpeT[:, NT:NK], p[0:32, 256:512], [p.b()], [kpeT.b(NT, NK)])
            for (t0, t1) in TT:
                pn = nps()
                for c in range(2):
                    s_ = ring("sq", sq16)
                    act(s_.ap, cqT[:, c, t0:t1], AF.Square, [cqT.b(c * NT + t0, c * NT + t1)], [s_.b()])
                    mm(pn.ap, ones16.ap, s_.ap, c == 0, c == 1, [ones16.b(), s_.b()], [pn.b()])
                rs = ring("rstd", rstd)
                rstd_from_psum(pn, 128, 256, rs)
                for c in range(2):
                    stt(cqn16[:, c, t0:t1], cqT[:, c, t0:t1], vcol(l, "g_q", c), rs.ap, ALU.mult, ALU.mult,
                        [cqT.b(c * NT + t0, c * NT + t1), vecT[l].b(), rs.b()], [cqn16.b(c * NT + t0, c * NT + t1)])
            for (k0, k1) in KT:
                n = k1 - k0
                pn = nps()
                s_ = ring("sq", sq16)
                act(s_[:, 0:n], ckvT[:, k0:k1], AF.Square, [ckvT.b(k0, k1)], [s_.b()])
                mm(pn[:, 0:n], ones16.ap, s_[:, 0:n], True, True, [ones16.b(), s_.b()], [pn.b()])
                rs = ring("rstd", rstd)
                act(rs[:, 0:n], pn[:, 0:n], AF.Ln, [pn.b(), epsc.b()], [rs.b()], scale=1.0 / 128, bias=epsc[:, 0:1])
                act(rs[:, 0:n], rs[:, 0:n], AF.Exp, [rs.b()], [rs.b()], scale=-0.5)
                stt(ckvn16[:, k0:k1], ckvT[:, k0:k1], vcol(l, "g_kv"), rs[:, 0:n], ALU.mult, ALU.mult,
                    [ckvT.b(k0, k1), vecT[l].b(), rs.b()], [ckvn16.b(k0, k1)])
            cp("dve", kpe16.ap, kpeT.ap, [kpeT.b()], [kpe16.b()])
            for kc in range(10):
                p = nps()
                mm(p.ap, ckvn16[:, kc * 128:(kc + 1) * 128], wv16.ap.rearrange("p h d -> p (h d)"), True, True,
                   [ckvn16.b(kc * 128, (kc + 1) * 128), wv16.b()], [p.b()])
                vdst = V16[:, kc, :].rearrange("p (j t d) -> p j t d", j=4, t=3, d=64)
                vsrc = p.ap.rearrange("p (j t d) -> p j t d", j=4, t=2, d=64)
                cp("act", vdst[:, :, 0, :], vsrc[:, :, 0, :], [p.b()], [V16.b(kc * 768, (kc + 1) * 768)])
                cp("act", vdst[:, :, 2, :], vsrc[:, :, 1, :], [p.b()], [V16.b(kc * 768, (kc + 1) * 768)])
            for (t0, t1) in TT:
                p = nps()
                mm(p[0:96, :], rot16.ap, kpe16[:, t0:t1], True, True, [rot16.b(), kpe16.b(t0, t1)], [p.b()])
                stt(SRk[:, t0:t1], p[0:96, :], vcol(l, "gsw_k", 0, 96), sinf[:, t0:t1], ALU.mult, ALU.mult,
                    [p.b(), vecT[l].b(), sinf.b()], [SRk.b(t0, t1)])
            stop("atprep")
            gq, gk, gswq = vcol(l, "g_qh", 0, 96), vcol(l, "g_kh", 0, 96), vcol(l, "gsw_q", 0, 96)
            rrmod[0] = 5
            pss = psb[5]

            def prep(h):
                kT = kT16[h % 2]
                qT = qT16[h % 2]
                e0 = h * 10
                for ki, (k0, k1) in enumerate(KT):
                    n = k1 - k0
                    p = nps()
                    mm(p[0:96, 0:n], wka16[:, h, :], ckvn16[:, k0:k1], True, False, [wka16.b(), ckvn16.b(k0, k1)], [p.b()])
                    mm(p[0:96, 0:n], ida16.ap, kpe16[:, k0:k1], False, True, [ida16.b(), kpe16.b(k0, k1)], [p.b()])
                    s_ = ring("sq", sq16)
                    act(s_[0:96, 0:n], p[0:96, 0:n], AF.Square, [p.b()], [s_.b()])
                    if ki < 2:
                        tf = ring("tmpf", tmpf)
                        stt(tf[0:96, :], p[0:96, :], gk, cosf[:, k0:k1], ALU.mult, ALU.mult,
                            [p.b(), vecT[l].b(), cosf.b()], [tf.b()])
                        tt("dve", kT[0:96, k0:k1], tf[0:96, :], SRk[:, k0:k1], ALU.add, [tf.b(), SRk.b(k0, k1)], [kT.b(k0, k1)])
                    else:
                        tsm(kT[0:96, k0:k1], p[0:96, 0:n], gk, [p.b(), vecT[l].b()], [kT.b(k0, k1)])
                    yield
                    for kk in range(n // 128):
                        col = (k0 // 128) + kk
                        mm(pss[:, col:col + 1], s_[0:96, kk * 128:(kk + 1) * 128], ones16[0:96, 0:1], True, True,
                           [s_.b(), ones16.b()], [pss.b(col, col + 1)])
                    yield
                act(esc[:, e0:e0 + 10], pss[:, 0:10], AF.Ln, [pss.b(0, 10), epsc.b()], [esc.b(e0, e0 + 10)],
                    scale=1.0 / 96, bias=epsc[:, 0:1])
                act(esc[:, e0:e0 + 10], esc[:, e0:e0 + 10], AF.Exp, [esc.b(e0, e0 + 10), lnsc.b()], [esc.b(e0, e0 + 10)],
                    scale=-0.5, bias=lnsc[:, 0:1])
                yield
                for (t0, t1) in TT:
                    pq, pr = nps(), nps()
                    for kc in range(2):
                        mm(pq[0:96, :], wq16[:, kc, h * 96:(h + 1) * 96], cqn16[:, kc, t0:t1], kc == 0, kc == 1,
                           [wq16.b(), cqn16.b(kc * NT + t0, kc * NT + t1)], [pq.b()])
                    for kc in range(2):
                        mm(pr[0:96, :], wqr16[:, kc, h * 96:(h + 1) * 96], cqn16[:, kc, t0:t1], kc == 0, kc == 1,
                           [wqr16.b(), cqn16.b(kc * NT + t0, kc * NT + t1)], [pr.b()])
                    s_ = ring("sq", sq16)
                    act(s_[0:96, :], pq[0:96, :], AF.Square, [pq.b()], [s_.b()])
                    tf, tf2 = ring("tmpf", tmpf), ring("tmpf", tmpf)
                    stt(tf[0:96, :], pq[0:96, :], gq, cosf[:, t0:t1], ALU.mult, ALU.mult, [pq.b(), vecT[l].b(), cosf.b()], [tf.b()])
                    stt(tf2[0:96, :], pr[0:96, :], gswq, sinf[:, t0:t1], ALU.mult, ALU.mult, [pr.b(), vecT[l].b(), sinf.b()], [tf2.b()])
                    yield
                    pn = nps()
                    mm(pn[0:96, :], ones16[0:96, 0:96], s_[0:96, :], True, True, [ones16.b(), s_.b()], [pn.b()])
                    tt("dve", tf[0:96, :], tf[0:96, :], tf2[0:96, :], ALU.add, [tf.b(), tf2.b()], [tf.b()])
                    yield
                    rs = ring("rstd", rstd)
                    rstd_from_psum(pn, 96, 96, rs)
                    tt("dve", qT[0:96, t0:t1], tf[0:96, :], rs[0:96, :], ALU.mult, [tf.b(), rs.b()], [qT.b(t0, t1)])
                    yield

            fin = [None]

            def scores(h, g, g2):
                kT = kT16[h % 2]
                qT = qT16[h % 2]
                e0 = h * 10
                pair, half = h // 2, (h % 2) * 64
                for ti, (t0, t1) in enumerate(TT):
                    po = psb[6 + (2 * h + ti) % 2]
                    oth = 64 - half

                    def smm(kc):
                        pS_ = nps_s()
                        mm(pS_.ap, kT[:, kc * 128:(kc + 1) * 128], qT[:, t0:t1], True, True,
                           [kT.b(kc * 128, (kc + 1) * 128), qT.b(t0, t1)], [pS_.b()])
                        return pS_
                    pq_ = [smm(0), smm(1)]
                    for kc in range(13):
                        if kc < 10:
                            if kc + 2 < 10:
                                pq_.append(smm(kc + 2))
                            pS = pq_[kc]
                            pt = PT16[kc % 5]
                            act(pt.ap, pS.ap, AF.Exp, [pS.b(), esc.b(e0 + kc, e0 + kc + 1)], [pt.b()],
                                scale=esc[:, e0 + kc:e0 + kc + 1])
                        if kc == 1 and fin[0] is not None:
                            fin[0]()
                            fin[0] = None
                        if kc >= 3:
                            kv = kc - 3
                            pt = PT16[kv % 5]
                            mm(po.ap, V16[:, kv, pair * 192 + half:pair * 192 + half + 128], pt.ap, kv == 0, kv == 9,
                               [V16.b(kv * 768, (kv + 1) * 768), pt.b()], [po.b()])
                        g = step(g)
                        g2[0] = step(g2[0])

                    def _fin(h=h, t0=t0, t1=t1, half=half, oth=oth, pair=pair, po=po):
                        r_ = rD[h % 2]
                        recip(r_[oth:oth + 64, :], po[oth:oth + 64, :], [po.b()], [r_.b()])
                        tt("dve", yT[half:half + 64, 4 + pair, t0:t1], po[half:half + 64, :], r_[oth:oth + 64, :], ALU.mult,
                           [po.b(), r_.b()], [yT.b((4 + pair) * NT + t0, (4 + pair) * NT + t1)])
                    fin[0] = _fin
                return g

            pool_lo[0], pool_n[0] = 3, 2
            drain(prep(0))
            def chain_mod():
                if l == 0 and gmod0[0] is not None:
                    holder0[0] = ws
                    yield from gmod0[0]
                    gmod0[0] = None
                if l + 1 < DEPTH and "one_layer" not in debug:
                    yield from mod_gen(l + 1, [ws])
            g2 = [chain_mod()]
            for h in range(8):
                g = prep(h + 1) if h < 7 else None
                g = scores(h, g, g2)
                drain(g)
            if fin[0] is not None:
                fin[0]()
                fin[0] = None
            drain(g2[0])
            pool_lo[0] = None
            dbg("yat%d" % l, yT, [128, 8, NT])
            stop("at")

            for (t0, t1) in TT:
                for (c0, c1) in ((0, 2), (2, 4), (4, 8)):
                    pn = nps()
                    for c in range(c0, c1):
                        s_ = ring("sq", sq16)
                        act(s_.ap, yT[:, c, t0:t1], AF.Square, [yT.b(c * NT + t0, c * NT + t1)], [s_.b()])
                        mm(pn.ap, ones16.ap, s_.ap, c == c0, c == c1 - 1, [ones16.b(), s_.b()], [pn.b()])
                    rs = ring("rstd", rstd)
                    rstd_from_psum(pn, 128, 128 * (c1 - c0), rs)
                    for c in range(c0, c1):
                        stt(yT[:, c, t0:t1], yT[:, c, t0:t1], vcol(l, "g_grp", c), rs.ap, ALU.mult, ALU.mult,
                            [yT.b(c * NT + t0, c * NT + t1), vecT[l].b(), rs.b()], [yT.b(c * NT + t0, c * NT + t1)])
            for (t0, t1) in TT:
                for dc in range(8):
                    p = nps()
                    for kc in range(8):
                        mm(p.ap, wov[:, kc, dc * 128:(dc + 1) * 128], yT[:, kc, t0:t1], kc == 0, kc == 7,
                           [wso.b(), yT.b(kc * NT + t0, kc * NT + t1)], [p.b()])
                    stt(xT[:, dc, t0:t1], p.ap, mT[:, 16 + dc:17 + dc], xT[:, dc, t0:t1], ALU.mult, ALU.add,
                        [p.b(), mT.b(16, 24), xT.b(dc * NT + t0, dc * NT + t1)], [xT.b(dc * NT + t0, dc * NT + t1)])
            dbg("xmid%d" % l, xT, [128, 8, NT])
            stop("wo")

            norm_mod(l, 1)
            car = Carver()
            actT = car.get([128, 22, NT], BF16)
            nblk = [(b * 512, min(512, DFF - b * 512)) for b in range(6)]
            for (f0, nf) in nblk:
                ws = wslot()
                wg = ws.ap[:, 0:8 * nf].rearrange("p (a b) -> p a b", a=8)
                wu = ws.ap[:, 4096:4096 + 8 * nf].rearrange("p (a b) -> p a b", a=8)
                dma("pool", wg, din["w_ff_gate"][l][:, f0:f0 + nf].rearrange("(kc p) n -> p kc n", p=128), writes=[ws.b(0, 4096)])
                dma("pool", wu, din["w_ff_up"][l][:, f0:f0 + nf].rearrange("(kc p) n -> p kc n", p=128), writes=[ws.b(4096, 8192)])
                for (t0, t1) in TT:
                    for fc in range(nf // 128):
                        f = f0 // 128 + fc
                        pg, pu = nps(), nps()
                        for kc in range(8):
                            mm(pg.ap, wg[:, kc, fc * 128:(fc + 1) * 128], hT[:, kc, t0:t1], kc == 0, kc == 7,
                               [ws.b(0, 4096), hT.b(kc * NT + t0, kc * NT + t1)], [pg.b()])
                        for kc in range(8):
                            mm(pu.ap, wu[:, kc, fc * 128:(fc + 1) * 128], hT[:, kc, t0:t1], kc == 0, kc == 7,
                               [ws.b(4096, 8192), hT.b(kc * NT + t0, kc * NT + t1)], [pu.b()])
                        tf = ring("tmpf", tmpf)
                        act(tf.ap, pg.ap, AF.Silu, [pg.b()], [tf.b()])
                        tt("dve", actT[:, f, t0:t1], tf.ap, pu.ap, ALU.mult, [tf.b(), pu.b()], [actT.b(f * NT + t0, f * NT + t1)])
            for db in range(4):
                ws = wslot()
                wd = ws.ap[:, 0:2 * 22 * 128].rearrange("p (d f m) -> p d f m", d=2, f=22)
                for dd in range(2):
                    dc = db * 2 + dd
                    dma("pool", wd[:, dd], din["w_ff_down"][l][:, dc * 128:(dc + 1) * 128].rearrange("(f p) m -> p f m", p=128),
                        writes=[ws.b(dd * 2816, (dd + 1) * 2816)])
                for dd in range(2):
                    dc = db * 2 + dd
                    for (t0, t1) in TT:
                        p = nps()
                        for f in range(22):
                            mm(p.ap, wd[:, dd, f, :], actT[:, f, t0:t1], f == 0, f == 21,
                               [ws.b(dd * 2816, (dd + 1) * 2816), actT.b(f * NT + t0, f * NT + t1)], [p.b()])
                        stt(xT[:, dc, t0:t1], p.ap, mT[:, 40 + dc:41 + dc], xT[:, dc, t0:t1], ALU.mult, ALU.add,
                            [p.b(), mT.b(40, 48), xT.b(dc * NT + t0, dc * NT + t1)], [xT.b(dc * NT + t0, dc * NT + t1)])
            dbg("xout%d" % l, xT, [128, 8, NT])

        m0c = sb([128, 8], F32, "m0c")
        dma("sp", m0c.ap, din["m0"], writes=[m0c.b()])
        nlayers = DEPTH if "one_layer" not in debug else 1
        holder0 = [wring[0]]
        gmod0 = [mod_gen(0, holder0)]
        for tc in range(8):
            for _ in range(3):
                gmod0[0] = step(gmod0[0])
            load_x(tc)
        wri[0] = 1
        try:
            for l in range(nlayers):
                layer(l)
        except _Stop:
            pass

        for tc in range(8):
            st_ = stage if tc % 2 == 0 else stage2
            for hh in range(2):
                p = nps()
                for q in range(4):
                    dc = hh * 4 + q
                    tr(p[:, q * 128:(q + 1) * 128], xT[:, dc, tc * 128:(tc + 1) * 128], ident.ap,
                       [xT.b(dc * NT + tc * 128, dc * NT + (tc + 1) * 128), ident.b()], [p.b(q * 128, (q + 1) * 128)])
                cp("act" if hh else "dve", st_[:, hh * 512:(hh + 1) * 512], p.ap, [p.b()], [st_.b(hh * 512, (hh + 1) * 512)])
            dma("sp", y_out[tc * 128:(tc + 1) * 128, :], st_.ap, reads=[st_.b()])

        P.emit(nc)
    return nc, dbg_out


_NC_CACHE = {}


def _in_maps(inputs):
    consts = {True: _core_consts(True), False: _core_consts(False)}
    f32 = np.float32
    maps = []
    for r in range(8):
        is_p = r < 4
        m = {}
        if is_p:
            m["x"] = np.ascontiguousarray(inputs["x_prompt"][4 * r:4 * r + 4].reshape(NT, D)).astype(f32, copy=False)
            cvec = np.asarray(inputs["c_ctx"], f32)
            m["cckv"] = np.zeros((2, 256, 128), f32)
            m["ckpe"] = np.zeros((2, 256, 32), f32)
        else:
            b = r - 4
            m["x"] = np.ascontiguousarray(inputs["x_sample"][b]).astype(f32, copy=False)
            cvec = np.asarray(inputs["c"][b], f32)
            m["cckv"] = np.ascontiguousarray(inputs["cache_ckv"][b]).astype(f32, copy=False)
            m["ckpe"] = np.ascontiguousarray(inputs["cache_kpe"][b]).astype(f32, copy=False)
        m["cvT"] = np.ascontiguousarray(cvec.reshape(8, 128).T)
        for n in W_NAMES:
            m[n] = np.ascontiguousarray(np.asarray(inputs[n], f32))
        m.update(consts[is_p])
        maps.append(m)
    return maps


def kernel(**inputs):
    inputs = {k: np.asarray(v) for k, v in inputs.items()}
    if "nc" not in _NC_CACHE:
        _NC_CACHE["nc"] = build_nc()[0]
    nc = _NC_CACHE["nc"]
    res = run_bass_kernel_spmd(nc, _in_maps(inputs), core_ids=list(range(8)))
    rs = res.results
    y_prompt = np.concatenate([rs[r]["y"].reshape(4, 256, D) for r in range(4)], axis=0).astype(np.float32)
    y_sample = np.stack([rs[r]["y"] for r in range(4, 8)], axis=0).astype(np.float32)
    new_ckv = np.concatenate([rs[r]["nckv"].reshape(2, 4, 256, 128).transpose(1, 0, 2, 3) for r in range(4)], axis=0)
    new_kpe = np.concatenate([rs[r]["nkpe"].reshape(2, 4, 256, 32).transpose(1, 0, 2, 3) for r in range(4)], axis=0)
    return (y_prompt, y_sample, np.ascontiguousarray(new_ckv, dtype=np.float32),
            np.ascontiguousarray(new_kpe, dtype=np.float32))
```

```python
import numpy as np
import ml_dtypes
from contextlib import ExitStack
import concourse.bass as bass
import concourse.mybir as mybir
from concourse.bass_utils import run_bass_kernel_spmd

F32 = mybir.dt.float32
BF16 = mybir.dt.bfloat16
AF = mybir.ActivationFunctionType
ALU = mybir.AluOpType

N_DMA_SEMS = 40
N_SW_SEMS = 12
D = 1024
NT = 1024
NK = 1280
DEPTH = 2
NIN = 1952
DFF = 2816
EPS = 1e-6
BIG = 30000.0
TT = [(0, 512), (512, 1024)]
KT = [(0, 512), (512, 1024), (1024, 1280)]


class Buf:
    __slots__ = ("key", "lo", "hi")

    def __init__(self, key, lo=0, hi=1 << 30):
        self.key, self.lo, self.hi = key, lo, hi


class _Op:
    __slots__ = ("eng", "fn", "deps", "signal", "sigval", "dma", "dsem", "dval", "idx")


class Prog:
    ENGS = ("pe", "act", "dve", "pool", "sp")

    def __init__(self):
        self.ops = []
        self.writes = {}
        self.reads = {}
        self.dma_count = 0
        self.dma_count_sw = 0
        self.dma_last = {}

    def _deps_for(self, op, reads, writes):
        deps = []
        is_dma = op.dma
        for b in reads:
            for (lo, hi, w) in self.writes.get(b.key, ()):
                if lo < b.hi and b.lo < hi:
                    deps.append(w)
            if b.key.startswith("ps"):
                for (lo, hi, r) in self.reads.get(b.key, ()):
                    if r.eng != op.eng:
                        deps.append(r)
        for b in writes:
            for (lo, hi, w) in self.writes.get(b.key, ()):
                if lo < b.hi and b.lo < hi:
                    if is_dma or w.dma or w.eng != op.eng or op.eng != "pe":
                        deps.append(w)
            for (lo, hi, r) in self.reads.get(b.key, ()):
                if lo < b.hi and b.lo < hi:
                    if is_dma or r.dma or r.eng != op.eng or op.eng != "pe":
                        deps.append(r)
        return deps

    def op(self, eng, fn, reads=(), writes=(), dma=False):
        reads = [Buf(b.key) if b.key.startswith("ps") else b for b in reads]
        writes = [Buf(b.key) if b.key.startswith("ps") else b for b in writes]
        o = _Op()
        o.eng, o.fn, o.dma = eng, fn, dma
        o.signal, o.sigval, o.dsem, o.dval = False, None, None, None
        o.idx = len(self.ops)
        o.deps = self._deps_for(o, reads, writes)
        if dma:
            if eng == "pool":
                base, npool = N_DMA_SEMS - N_SW_SEMS, N_SW_SEMS
                c = self.dma_count_sw
                self.dma_count_sw += 1
            else:
                base, npool = 0, N_DMA_SEMS - N_SW_SEMS
                c = self.dma_count
                self.dma_count += 1
            s = base + c % npool
            u = c // npool
            o.dsem, o.dval = s, 16 * (u + 1)
            prev = self.dma_last.get(s)
            if prev is not None:
                o.deps.append(prev)
            self.dma_last[s] = o
        for b in writes:
            wl = self.writes.setdefault(b.key, [])
            wl[:] = [e for e in wl if not (b.lo <= e[0] and e[1] <= b.hi)]
            wl.append((b.lo, b.hi, o))
            rl = self.reads.get(b.key)
            if rl:
                rl[:] = [e for e in rl if not (b.lo <= e[0] and e[1] <= b.hi)]
        for b in reads:
            rl = self.reads.setdefault(b.key, [])
            if not dma:
                rl[:] = [e for e in rl if not (e[0] == b.lo and e[1] == b.hi
                                               and (not e[2].dma) and e[2].eng == eng)]
            rl.append((b.lo, b.hi, o))
        self.ops.append(o)
        return o

    def emit(self, nc, final_wait_eng="sp"):
        ops = self.ops
        for o in ops:
            for d in o.deps:
                if not d.dma:
                    d.signal = True
        cnt = {e: 0 for e in self.ENGS}
        for o in ops:
            if o.signal:
                cnt[o.eng] += 1
                o.sigval = cnt[o.eng]
        per_eng = {e: [] for e in self.ENGS}
        for o in ops:
            per_eng[o.eng].append(o)
        self.stats = (dict(cnt), self.dma_count, len(ops))
        with ExitStack() as es:
            esem = {e: es.enter_context(nc.semaphore("c_" + e)) for e in self.ENGS}
            dsem = [es.enter_context(nc.semaphore("d_%d" % i)) for i in range(N_DMA_SEMS)]
            block = es.enter_context(nc.Block())

            def run(engname, e):
                waited = {}
                for o in per_eng[engname]:
                    need = {}
                    for d in o.deps:
                        if d.dma:
                            k, v = ("d", d.dsem), d.dval
                        else:
                            k, v = ("e", d.eng), d.sigval
                        if waited.get(k, 0) < v and need.get(k, 0) < v:
                            need[k] = v
                    for k, v in need.items():
                        sem = dsem[k[1]] if k[0] == "d" else esem[k[1]]
                        e.wait_ge(sem, v)
                        waited[k] = v
                    ins = o.fn(e)
                    if o.dma:
                        ins.then_inc(dsem[o.dsem], 16)
                    elif o.signal:
                        ins.then_inc(esem[o.eng], 1)
                if engname == final_wait_eng:
                    for s, last in self.dma_last.items():
                        if waited.get(("d", s), 0) < last.dval:
                            e.wait_ge(dsem[s], last.dval)

            @block.tensor
            def _(e):
                run("pe", e)

            @block.scalar
            def _(e):
                run("act", e)

            @block.vector
            def _(e):
                run("dve", e)

            @block.gpsimd
            def _(e):
                run("pool", e)

            @block.sync
            def _(e):
                run("sp", e)


class T:
    def __init__(self, ap, key, base, esize, n):
        self.ap, self.key, self.base, self.esize, self.n = ap, key, base, esize, n

    def b(self, lo=None, hi=None):
        lo = 0 if lo is None else lo
        hi = self.n if hi is None else hi
        return Buf(self.key, self.base + lo * self.esize, self.base + hi * self.esize)

    def __getitem__(self, k):
        return self.ap[k]


def _core_consts(is_prompt):
    f32 = np.float32
    L, nblk = (256, 4) if is_prompt else (1024, 1)
    c = {}
    t = np.linspace(0.0, 1.0, L, dtype=f32)[:, None]
    w_ang = (2.0 * np.pi * np.arange(L, dtype=f32)[:, None] / L).astype(f32)
    bands = np.linspace(1e-4, 15.0, 16, dtype=f32)[None, :]
    z = np.concatenate([t, np.cos(bands * w_ang), -np.sin(bands * w_ang)], axis=-1).astype(f32)
    c["zembT"] = np.ascontiguousarray(np.tile(z, (nblk, 1)).T)
    deltas = np.abs(np.linspace(np.log(1e-2) / 0.3, np.log(1e-2) / 1.5, 256, dtype=f32))
    win = (np.exp(-t * deltas[None, :]) + 0.05).astype(f32)
    c["window"] = np.ascontiguousarray(np.tile(win, (nblk, 1)))
    m0 = np.ones(1024, f32)
    m0[::L] = 0.0
    c["m0"] = np.ascontiguousarray(m0.reshape(8, 128).T)
    s = np.arange(L, dtype=np.float64)[:, None]
    f = np.arange(L, dtype=np.float64)[None, :]
    ang = np.pi * (f + 0.5) * s / L
    Cb, Sb = np.cos(ang), -np.sin(ang)
    Cf = np.zeros((1024, 1024)); Sf = np.zeros((1024, 1024))
    for b in range(nblk):
        Cf[b * L:(b + 1) * L, b * L:(b + 1) * L] = Cb
        Sf[b * L:(b + 1) * L, b * L:(b + 1) * L] = Sb
    fb = np.stack([Cf, Sf], 0).reshape(2, 8, 128, 8, 128)
    c["fwdB"] = np.ascontiguousarray(fb.transpose(3, 2, 0, 1, 4)).astype(ml_dtypes.bfloat16)
    ib = np.stack([Cf.T / L, Sf.T / L], 0).reshape(2, 8, 128, 1024)
    c["invB"] = np.ascontiguousarray(ib.transpose(1, 2, 0, 3)).astype(ml_dtypes.bfloat16)
    cosf = np.ones((96, 1024), f32); sinf = np.zeros((96, 1024), f32)
    if not is_prompt:
        tok = np.arange(1024)
        row = (tok // 64).astype(f32); col = (tok % 64).astype(f32)
        inv = (1.0 / (10000.0 ** (np.arange(0, 16, 2, dtype=f32) / 16.0))).astype(f32)
        angr = np.concatenate([row[:, None] * inv, col[:, None] * inv], -1).astype(f32)
        cs, sn = np.cos(angr), np.sin(angr)
        for ax in range(2):
            for ab in range(2):
                for i in range(8):
                    d = 64 + ax * 16 + ab * 8 + i
                    cosf[d] = cs[:, ax * 8 + i]
                    sinf[d] = sn[:, ax * 8 + i]
    c["cosf"], c["sinf"] = cosf, sinf
    qm = np.zeros((4, 1024), f32); km = np.zeros((4, NK), f32)
    if is_prompt:
        for e in range(4):
            qm[e, e * 256:(e + 1) * 256] = BIG
            km[e, :] = -1.0
            km[e, e * 256:(e + 1) * 256] = 0.0
    c["qm"] = qm.astype(ml_dtypes.bfloat16)
    c["km"] = km.astype(ml_dtypes.bfloat16)
    c["nb"] = np.full((128, 1), -1.0 if is_prompt else 0.0, f32)
    rot = np.zeros((32, 96), f32); ida = np.zeros((32, 96), f32)
    for ax in range(2):
        for i in range(8):
            a, b = ax * 16 + i, ax * 16 + 8 + i
            rot[b, 64 + a] = -1.0
            rot[a, 64 + b] = 1.0
    for k in range(32):
        ida[k, 64 + k] = 1.0
    c["rot"] = rot.astype(ml_dtypes.bfloat16)
    c["ida"] = ida.astype(ml_dtypes.bfloat16)
    c["ident"] = np.eye(128, dtype=f32)
    return c


W_NAMES = ["w_mod", "b_mod", "g_norm1", "w_in", "hy_conv", "hy_w1", "hy_b1", "hy_freq", "hy_w2", "hy_b2",
           "hy_w3", "hy_bias", "sc_conv", "g_q", "w_uq", "g_kv", "w_ukv", "g_qh", "g_kh", "g_grp", "w_out",
           "g_norm2", "w_ff_gate", "w_ff_up", "w_ff_down"]
W_SHAPES = {"w_mod": [2, 1024, 6144], "b_mod": [2, 6144], "g_norm1": [2, 1024], "w_in": [2, 1024, 1952],
            "hy_conv": [2, 3, 768], "hy_w1": [2, 33, 64], "hy_b1": [2, 64], "hy_freq": [2, 64],
            "hy_w2": [2, 64, 64], "hy_b2": [2, 64], "hy_w3": [2, 64, 512], "hy_bias": [2, 256],
            "sc_conv": [2, 3, 256], "g_q": [2, 256], "w_uq": [2, 256, 768], "g_kv": [2, 128],
            "w_ukv": [2, 128, 1024], "g_qh": [2, 96], "g_kh": [2, 96], "g_grp": [2, 1024],
            "w_out": [2, 1024, 1024], "g_norm2": [2, 1024], "w_ff_gate": [2, 1024, 2816],
            "w_ff_up": [2, 1024, 2816], "w_ff_down": [2, 2816, 1024]}
C_SHAPES = {"zembT": ([33, 1024], F32), "window": ([1024, 256], F32), "m0": ([128, 8], F32),
            "fwdB": ([8, 128, 2, 8, 128], BF16), "invB": ([8, 128, 2, 1024], BF16),
            "cosf": ([96, 1024], F32), "sinf": ([96, 1024], F32), "qm": ([4, 1024], BF16),
            "km": ([4, NK], BF16), "nb": ([128, 1], F32), "rot": ([32, 96], BF16), "ida": ([32, 96], BF16),
            "ident": ([128, 128], F32)}


def build_nc(debug=()):
    nc = bass.Bass("TRN2", target_bir_lowering=False)
    din = {}
    din["x"] = nc.dram_tensor("x", [NT, D], F32, kind="ExternalInput").ap()
    din["cvT"] = nc.dram_tensor("cvT", [128, 8], F32, kind="ExternalInput").ap()
    din["cckv"] = nc.dram_tensor("cckv", [2, 256, 128], F32, kind="ExternalInput").ap()
    din["ckpe"] = nc.dram_tensor("ckpe", [2, 256, 32], F32, kind="ExternalInput").ap()
    for n in W_NAMES:
        din[n] = nc.dram_tensor(n, W_SHAPES[n], F32, kind="ExternalInput").ap()
    for n, (shp, dt) in C_SHAPES.items():
        din[n] = nc.dram_tensor(n, shp, dt, kind="ExternalInput").ap()
    y_out = nc.dram_tensor("y", [NT, D], F32, kind="ExternalOutput").ap()
    nckv_out = nc.dram_tensor("nckv", [2, NT, 128], F32, kind="ExternalOutput").ap()
    nkpe_out = nc.dram_tensor("nkpe", [2, NT, 32], F32, kind="ExternalOutput").ap()
    dbg_out = {}

    P = Prog()
    es = ExitStack()
    with es:
        uid = [0]

        def sb(shape, dt, name=None):
            uid[0] += 1
            name = "s_" + (name or ("t%d" % uid[0]))
            h = es.enter_context(nc.sbuf_tensor(name, shape, dt))
            n = int(np.prod(shape[1:]))
            return T(h.ap(), name, 0, 2 if dt == BF16 else 4, n)

        ARENA_F32 = 12800
        arena_h = es.enter_context(nc.sbuf_tensor("arena", [128, ARENA_F32], F32))
        g_h = [es.enter_context(nc.sbuf_tensor("G%d" % i, [128, 2048], F32)) for i in range(3)]

        class Carver:
            def __init__(self, handle=None, key="arena", size=None):
                self.off = 0
                self.h = arena_h if handle is None else handle
                self.key = key
                self.size = ARENA_F32 * 4 if size is None else size

            def get(self, shape, dt):
                es_ = 2 if dt == BF16 else 4
                n = int(np.prod(shape[1:]))
                nb = (n * es_ + 3) // 4 * 4
                lo = self.off
                self.off += nb
                assert self.off <= self.size, "arena overflow %s %d" % (self.key, self.off)
                ap = self.h[:, lo // 4:(lo + nb) // 4]
                if dt == BF16:
                    ap = ap.bitcast(BF16)
                    if n * 2 < nb:
                        ap = ap[:, 0:n]
                if len(shape) == 3:
                    ap = ap.rearrange("p (a b) -> p a b", a=shape[1])
                if shape[0] < 128:
                    ap = ap[0:shape[0]]
                return T(ap, self.key, lo, es_, n)

        psb = []
        for i in range(8):
            h = es.enter_context(nc.psum_tensor("ps%d" % i, [128, 512], F32))
            psb.append(T(h[:], "ps%d" % i, 0, 4, 512))
        psi = [0]
        rrmod = [5]

        def nps():
            if pool_lo[0] is None:
                p = psb[psi[0] % rrmod[0]]
                psi[0] += 1
            else:
                p = psb[pool_lo[0] + psi2[0] % pool_n[0]]
                psi2[0] += 1
            return p
        pool_lo, pool_n, psi2 = [None], [0], [0]
        psi3 = [0]

        def nps_s():
            p = psb[psi3[0] % 3]
            psi3[0] += 1
            return p

        def dma(eng, out, in_, reads=(), writes=(), **kw):
            return P.op(eng, lambda e: e.dma_start(out=out, in_=in_, **kw), reads=reads, writes=writes, dma=True)

        def mm(out, lhsT, rhs, start, stop, reads, writes):
            return P.op("pe", lambda e: e.matmul(out, lhsT=lhsT, rhs=rhs, start=start, stop=stop),
                        reads=reads, writes=writes)

        def tr(out, in_, ident, reads, writes):
            return P.op("pe", lambda e: e.transpose(out, in_, ident), reads=reads, writes=writes)

        def act(out, in_, func, reads, writes, scale=1.0, bias=0.0):
            return P.op("act", lambda e: e.activation(out=out, in_=in_, func=func, bias=bias, scale=scale),
                        reads=reads, writes=writes)

        def tt(eng, out, a, b, op, reads, writes):
            return P.op(eng, lambda e: e.tensor_tensor(out=out, in0=a, in1=b, op=op), reads=reads, writes=writes)

        def ts(out, a, s1, s2, op0, op1, reads, writes):
            return P.op("dve", lambda e: e.tensor_scalar(out, a, s1, s2, op0, op1), reads=reads, writes=writes)

        def tsm(out, a, s1, reads, writes):
            return P.op("dve", lambda e: e.tensor_scalar_mul(out, a, s1), reads=reads, writes=writes)

        def stt(out, a, s, b, op0, op1, reads, writes):
            return P.op("dve", lambda e: e.scalar_tensor_tensor(out=out, in0=a, scalar=s, in1=b, op0=op0, op1=op1),
                        reads=reads, writes=writes)

        def cp(eng, out, in_, reads, writes):
            if eng == "act":
                return act(out, in_, AF.Copy, reads, writes)
            return P.op(eng, lambda e: e.tensor_copy(out=out, in_=in_), reads=reads, writes=writes)

        def recip(out, in_, reads, writes):
            return P.op("dve", lambda e: e.reciprocal(out, in_), reads=reads, writes=writes)

        def memset(eng, t_, val, ap=None):
            a = t_.ap if ap is None else ap
            return P.op(eng, lambda e: e.memset(a, val), writes=[t_.b()])

        def dbg(name, t_, shape):
            if name in debug:
                o = nc.dram_tensor("dbg_" + name, shape, F32 if t_.esize == 4 else BF16, kind="ExternalOutput").ap()
                dbg_out[name] = o
                dma("sp", o, t_.ap, reads=[t_.b()])

        xT = sb([128, 8, NT], F32, "xT")
        hT = sb([128, 8, NT], BF16, "hT")
        yT = sb([128, 8, NT], BF16, "yT")
        wring = [sb([128, 8192], BF16, "wr0"), sb([128, 8192], BF16, "wr1")]
        wri = [0]

        def wslot():
            s_ = wring[wri[0] % 2]
            wri[0] += 1
            return s_

        ident = sb([128, 128], F32, "ident")
        ident16 = sb([128, 128], BF16, "ident16")
        ones16 = sb([128, 128], BF16, "ones16")
        epsc = sb([128, 1], F32, "epsc")
        lnsc = sb([128, 1], F32, "lnsc")
        nbc = sb([128, 1], F32, "nbc")
        cosf = sb([96, NT], F32, "cosf")
        sinf = sb([96, NT], F32, "sinf")
        vstage = [sb([128, 128], F32, "vst%d" % l) for l in range(DEPTH)]
        vecT = [sb([128, 128], F32, "vecT%d" % l) for l in range(DEPTH)]
        modT = [sb([128, 48], F32, "modT%d" % l) for l in range(DEPTH)]
        gm = [sb([128, 16], F32, "gm%d" % l) for l in range(DEPTH)]
        nbw = [sb([128, 24], F32, "nbw%d" % l) for l in range(DEPTH)]
        frb = [sb([64, 8], F32, "frb%d" % l) for l in range(DEPTH)]
        scv16 = sb([128, 8], BF16, "scv16")
        cv = sb([128, 8], F32, "cv")
        rstd = [sb([128, 512], F32, "rstd%d" % i) for i in range(2)]
        tmpf = [sb([128, 512], F32, "tmpf%d" % i) for i in range(3)]
        sq16 = [sb([128, 512], BF16, "sq16_%d" % i) for i in range(3)]
        cnt = {"rstd": 0, "tmpf": 0, "sq": 0}

        def ring(name, lst):
            t_ = lst[cnt[name] % len(lst)]
            cnt[name] += 1
            return t_

        G = [T(g_h[i].ap().rearrange("p (a b) -> p a b", a=2), "G%d" % i, 0, 4, 2048) for i in range(3)]
        upad = sb([128, NT + 2], F32, "upad")
        stage = sb([128, 1024], F32, "stage")
        convo = stage

        dma("sp", ident.ap, din["ident"], writes=[ident.b()])
        cp("dve", ident16.ap, ident.ap, [ident.b()], [ident16.b()])
        memset("dve", ones16, 1.0)
        memset("dve", epsc, EPS)
        memset("dve", lnsc, float(np.log(96.0 ** -0.5)))
        memset("dve", upad, 0.0)
        dma("sp", nbc.ap, din["nb"], writes=[nbc.b()])
        dma("sp", cosf.ap, din["cosf"], writes=[cosf.b()])
        dma("sp", sinf.ap, din["sinf"], writes=[sinf.b()])
        dma("sp", cv.ap, din["cvT"], writes=[cv.b()])

        VOFF = {}
        r = 0
        for name, nrows in [("b_mod", 48), ("g_norm1", 8), ("g_norm2", 8), ("g_grp", 8), ("hy_conv", 18),
                            ("sc_conv", 6), ("hy_bias", 2), ("g_q", 2), ("g_kv", 1), ("g_qh", 1), ("g_kh", 1),
                            ("gsw_q", 1), ("gsw_k", 1), ("hy_b1", 1), ("hy_freq", 1), ("hy_b2", 1)]:
            VOFF[name] = r
            r += nrows
        assert r <= 128
        for l in range(DEPTH):
            vs = vstage[l]
            pend = []

            def vrow(name, src, nrows, n, c0=0, roff=0):
                r0 = VOFF[name] + roff
                pend.append((vs[r0:r0 + nrows, c0:c0 + n], src))
            vrow("b_mod", din["b_mod"][l].rearrange("(r p) -> r p", p=128), 48, 128)
            vrow("g_norm1", din["g_norm1"][l].rearrange("(r p) -> r p", p=128), 8, 128)
            vrow("g_norm2", din["g_norm2"][l].rearrange("(r p) -> r p", p=128), 8, 128)
            vrow("g_grp", din["g_grp"][l].rearrange("(r p) -> r p", p=128), 8, 128)
            for k in range(3):
                vrow("hy_conv", din["hy_conv"][l, k].rearrange("(c p) -> c p", p=128), 6, 128, roff=6 * k)
                vrow("sc_conv", din["sc_conv"][l, k].rearrange("(c p) -> c p", p=128), 2, 128, roff=2 * k)
            vrow("hy_bias", din["hy_bias"][l].rearrange("(r p) -> r p", p=128), 2, 128)
            vrow("g_q", din["g_q"][l].rearrange("(r p) -> r p", p=128), 2, 128)
            vrow("g_kv", din["g_kv"][l:l + 1, :], 1, 128)
            vrow("g_qh", din["g_qh"][l:l + 1, :], 1, 96)
            vrow("g_kh", din["g_kh"][l:l + 1, :], 1, 96)
            for nm, src in (("gsw_q", "g_qh"), ("gsw_k", "g_kh")):
                for ax in range(2):
                    a0 = 64 + ax * 16
                    vrow(nm, din[src][l:l + 1, a0 + 8:a0 + 16], 1, 8, c0=a0)
                    vrow(nm, din[src][l:l + 1, a0:a0 + 8], 1, 8, c0=a0 + 8)
            vrow("hy_b1", din["hy_b1"][l:l + 1, :], 1, 64)
            vrow("hy_freq", din["hy_freq"][l:l + 1, :], 1, 64)
            vrow("hy_b2", din["hy_b2"][l:l + 1, :], 1, 64)
            vkeys = [Buf("vsk%d_%d" % (l, i)) for i in range(len(pend))]
            P.op("dve", lambda e, a=vs.ap: e.memset(a, 0.0), writes=[vs.b()] + vkeys)
            for (dst, src), kb in zip(pend, vkeys):
                dma("sp", dst, src, writes=[kb])
            vs_all = [vs.b()] + vkeys
            p = nps()
            tr(p[:, 0:128], vs.ap, ident.ap, vs_all + [ident.b()], [p.b(0, 128)])
            cp("dve", vecT[l].ap, p[:, 0:128], [p.b(0, 128)], [vecT[l].b()])
            c0 = VOFF["hy_conv"]
            tsm(nbw[l].ap, vecT[l][:, c0:c0 + 24], nbc[:, 0:1], [vecT[l].b(), nbc.b()], [nbw[l].b()])
            vT = vecT[l]
            cf, cb1, cb2 = VOFF["hy_freq"], VOFF["hy_b1"], VOFF["hy_b2"]
            tsm(frb[l][:, 0:1], vT[0:64, cf:cf + 1], 0.25, [vT.b()], [frb[l].b()])
            tt("dve", frb[l][:, 1:2], frb[l][:, 0:1], vT[0:64, cb1:cb1 + 1], ALU.mult, [vT.b(), frb[l].b()], [frb[l].b()])
            tt("dve", frb[l][:, 2:3], frb[l][:, 0:1], vT[0:64, cb2:cb2 + 1], ALU.mult, [vT.b(), frb[l].b()], [frb[l].b()])
            tsm(frb[l][:, 3:6], frb[l][:, 0:3], 0.5, [frb[l].b()], [frb[l].b()])

        def vcol(l, name, j=0, n=128):
            c0 = VOFF[name] + j
            return vecT[l][0:n, c0:c0 + 1]

        act(cv.ap, cv.ap, AF.Silu, [cv.b()], [cv.b()])
        cp("dve", scv16.ap, cv.ap, [cv.b()], [scv16.b()])

        stage2 = T(g_h[0].ap()[:, 0:1024], "G0", 0, 4, 1024)

        def load_x(tc):
            st_ = stage if tc % 2 == 0 else stage2
            dma("sp", st_.ap, din["x"][tc * 128:(tc + 1) * 128, :], writes=[st_.b()])
            for hh in range(2):
                p = nps()
                for q in range(4):
                    dc = hh * 4 + q
                    tr(p[:, q * 128:(q + 1) * 128], st_[:, dc * 128:(dc + 1) * 128], ident.ap,
                       [st_.b(), ident.b()], [p.b(q * 128, (q + 1) * 128)])
                cp("act" if hh else "dve", xT[:, hh * 4:hh * 4 + 4, tc * 128:(tc + 1) * 128],
                   p.ap.rearrange("p (a b) -> p a b", a=4), [p.b()],
                   [xT.b(dc_ * NT + tc * 128, dc_ * NT + (tc + 1) * 128) for dc_ in range(hh * 4, hh * 4 + 4)])

        def step(g):
            if g is not None:
                try:
                    next(g)
                except StopIteration:
                    return None
            return g

        def drain(g):
            while g is not None:
                g = step(g)

        def mod_gen(l, holder):
            pm = psb[5]
            for pc in range(12):
                slot = holder[0]
                halves = [(slot.ap[:, 0:4096].rearrange("p (a b) -> p a b", a=8), slot.b(0, 4096)),
                          (slot.ap[:, 4096:8192].rearrange("p (a b) -> p a b", a=8), slot.b(4096, 8192))]
                wv, wb = halves[pc % 2]
                dma("pool", wv, din["w_mod"][l][:, pc * 512:(pc + 1) * 512].rearrange("(kc p) n -> p kc n", p=128),
                    writes=[wb])
                for jj in range(4):
                    j = pc * 4 + jj
                    for kc in range(8):
                        mm(pm[:, 16 + j:17 + j], wv[:, kc, jj * 128:(jj + 1) * 128], scv16[:, kc:kc + 1],
                           kc == 0, kc == 7, [wb, scv16.b()], [pm.b()])
                    yield
                c0 = VOFF["b_mod"] + 4 * pc
                tt("dve", modT[l][:, 4 * pc:4 * pc + 4], pm[:, 16 + 4 * pc:20 + 4 * pc], vecT[l][:, c0:c0 + 4], ALU.add,
                   [pm.b(), vecT[l].b()], [modT[l].b(4 * pc, 4 * pc + 4)])
                for i, (nm, sc0, pcd) in enumerate((("g_norm1", 8, 3), ("g_norm2", 32, 9))):
                    if pc == pcd:
                        g0 = VOFF[nm]
                        stt(gm[l][:, 8 * i:8 * i + 8], modT[l][:, sc0:sc0 + 8], 1.0, vecT[l][:, g0:g0 + 8], ALU.add, ALU.mult,
                            [modT[l].b(sc0, sc0 + 8), vecT[l].b()], [gm[l].b(8 * i, 8 * i + 8)])
                yield

        def rstd_from_psum(pn, nparts, nfeat, out_t):
            act(out_t[0:nparts, :], pn[0:nparts, :], AF.Ln, [pn.b(), epsc.b()], [out_t.b()],
                scale=1.0 / nfeat, bias=epsc[0:nparts, 0:1])
            act(out_t[0:nparts, :], out_t[0:nparts, :], AF.Exp, [out_t.b()], [out_t.b()], scale=-0.5)

        def norm_mod(l, which):
            gofs = 8 * which
            shofs = 0 if which == 0 else 24
            for (t0, t1) in TT:
                pn = nps()
                for dc in range(8):
                    s_ = ring("sq", sq16)
                    act(s_.ap, xT[:, dc, t0:t1], AF.Square, [xT.b(dc * NT + t0, dc * NT + t1)], [s_.b()])
                    mm(pn.ap, ones16.ap, s_.ap, dc == 0, dc == 7, [ones16.b(), s_.b()], [pn.b()])
                rs = ring("rstd", rstd)
                rstd_from_psum(pn, 128, D, rs)
                for dc in range(8):
                    tf = ring("tmpf", tmpf)
                    stt(tf.ap, xT[:, dc, t0:t1], gm[l][:, gofs + dc:gofs + dc + 1], rs.ap, ALU.mult, ALU.mult,
                        [xT.b(dc * NT + t0, dc * NT + t1), gm[l].b(gofs, gofs + 8), rs.b()], [tf.b()])
                    act(hT[:, dc, t0:t1], tf.ap, AF.Identity, [tf.b(), modT[l].b(shofs, shofs + 8)],
                        [hT.b(dc * NT + t0, dc * NT + t1)], bias=modT[l][:, shofs + dc:shofs + dc + 1])

        def proj_chunk(ws, wv, col0, m, t0, t1):
            p = nps()
            for kc in range(8):
                mm(p[0:m, 0:t1 - t0], wv[:, kc, col0:col0 + m], hT[:, kc, t0:t1], kc == 0, kc == 7,
                   [ws.b(), hT.b(kc * NT + t0, kc * NT + t1)], [p.b()])
            return p

        def load_win_piece(l, c0, ncols):
            ws = wslot()
            wv = ws.ap[:, 0:8 * ncols].rearrange("p (a b) -> p a b", a=8)
            dma("pool", wv, din["w_in"][l][:, c0:c0 + ncols].rearrange("(kc p) n -> p kc n", p=128), writes=[ws.b()])
            return ws, wv

        def conv3(l, wname, widx, nbidx, dst_ap, dst_bufs):
            w = [vcol(l, wname, widx[k]) for k in range(3)]
            rb = [upad.b(), vecT[l].b()]
            tsm(dst_ap, upad[:, 1:NT + 1], w[1], rb, dst_bufs)
            stt(dst_ap, upad[:, 0:NT], w[0], dst_ap, ALU.mult, ALU.add, rb + dst_bufs, dst_bufs)
            stt(dst_ap, upad[:, 2:NT + 2], w[2], dst_ap, ALU.mult, ALU.add, rb + dst_bufs, dst_bufs)
            n0 = nbw[l][:, nbidx[0]:nbidx[0] + 1]
            n2 = nbw[l][:, nbidx[2]:nbidx[2] + 1]
            rb2 = [upad.b(), nbw[l].b()]
            stt(dst_ap[:, 256:1024:256], upad[:, 256:1024:256], n0, dst_ap[:, 256:1024:256], ALU.mult, ALU.add,
                rb2 + dst_bufs, dst_bufs)
            stt(dst_ap[:, 255:1023:256], upad[:, 257:1025:256], n2, dst_ap[:, 255:1023:256], ALU.mult, ALU.add,
                rb2 + dst_bufs, dst_bufs)

        def sin_mlp(pre, nparts, l, bcol, out_t, t0, t1, car):
            s4, s8, c4 = car["s4"], car["s8"], car["c4"]
            n = t1 - t0
            act(s4[0:nparts, 0:n], pre, AF.Sin, [car["pre"].b(), frb[l].b()], [s4.b()],
                scale=frb[l][0:nparts, 0:1], bias=frb[l][0:nparts, bcol:bcol + 1])
            act(s8[0:nparts, 0:n], pre, AF.Sin, [car["pre"].b(), frb[l].b()], [s8.b()],
                scale=frb[l][0:nparts, 3:4], bias=frb[l][0:nparts, bcol + 3:bcol + 4])
            tt("dve", c4[0:nparts, 0:n], s8[0:nparts, 0:n], s8[0:nparts, 0:n], ALU.mult, [s8.b()], [c4.b()])
            ts(c4[0:nparts, 0:n], c4[0:nparts, 0:n], -2.0, 1.0, ALU.mult, ALU.add, [c4.b()], [c4.b()])
            stt(s8[0:nparts, 0:n], s4[0:nparts, 0:n], 2.0, c4[0:nparts, 0:n], ALU.mult, ALU.mult, [s4.b(), c4.b()], [s8.b()])
            tt("dve", c4[0:nparts, 0:n], s4[0:nparts, 0:n], s4[0:nparts, 0:n], ALU.mult, [s4.b()], [c4.b()])
            ts(c4[0:nparts, 0:n], c4[0:nparts, 0:n], -2.0, 1.0, ALU.mult, ALU.add, [c4.b()], [c4.b()])
            stt(out_t[0:nparts, t0:t1], s8[0:nparts, 0:n], 2.0, c4[0:nparts, 0:n], ALU.mult, ALU.mult,
                [s8.b(), c4.b()], [out_t.b(t0, t1)])

        class _Stop(Exception):
            pass

        def stop(name):
            if ("stop:" + name) in debug:
                raise _Stop()

        def layer(l):
            mT = modT[l]
            stop("s0")
            norm_mod(l, 0)
            dbg("hT%d" % l, hT, [128, 8, NT])
            stop("norm")

            car = Carver()
            x0T, x1T, zT = G[0], G[1], G[2]
            z16T = car.get([128, 2, NT], BF16)
            ztok = car.get([128, 8, 256], BF16)
            hp = car.get([128, 8, 256], BF16)
            hm = car.get([128, 8, 256], BF16)
            Yt = car.get([128, 8, 512], BF16)
            t1f = car.get([128, 256], F32)
            t2f = car.get([128, 256], F32)
            mark = car.off
            h1T = car.get([64, NT], F32)
            h2T = car.get([64, NT], BF16)
            s4 = car.get([64, 512], F32)
            s8 = car.get([64, 512], F32)
            c4 = car.get([64, 512], F32)
            wnd = [car.get([128, 256], F32) for _ in range(2)]
            w1s = car.get([33, 64], F32)
            w2s = car.get([64, 64], F32)
            w3s = car.get([64, 512], BF16)
            zemb = car.get([33, NT], F32)
            car.off = mark
            dring = [car.get([128, 2, 1024], BF16) for _ in range(2)]
            Ksb = car.get([128, 512], F32)

            def filt_gen():
                dma("sp", w1s.ap, din["hy_w1"][l], writes=[w1s.b()])
                dma("sp", w2s.ap, din["hy_w2"][l], writes=[w2s.b()])
                dma("pool", w3s.ap, din["hy_w3"][l], writes=[w3s.b()])
                dma("sp", zemb.ap, din["zembT"], writes=[zemb.b()])
                carry = {"s4": s4, "s8": s8, "c4": c4}
                for (t0, t1) in TT:
                    p = nps()
                    mm(p[0:64, :], w1s.ap, zemb[:, t0:t1], True, True, [w1s.b(), zemb.b()], [p.b()])
                    carry["pre"] = p
                    sin_mlp(p[0:64, :], 64, l, 1, h1T, t0, t1, carry)
                    yield
                for (t0, t1) in TT:
                    p = nps()
                    mm(p[0:64, :], w2s.ap, h1T[:, t0:t1], True, True, [w2s.b(), h1T.b(t0, t1)], [p.b()])
                    carry["pre"] = p
                    sin_mlp(p[0:64, :], 64, l, 2, h2T, t0, t1, carry)
                    yield
                for tcx in range(8):
                    wn = wnd[tcx % 2]
                    dma("sp", wn.ap, din["window"][tcx * 128:(tcx + 1) * 128, :], writes=[wn.b()])
                    p = nps()
                    mm(p.ap, h2T[:, tcx * 128:(tcx + 1) * 128], w3s.ap, True, True, [h2T.b(), w3s.b()], [p.b()])
                    tt("dve", t1f.ap, p[:, 0:256], wn.ap, ALU.mult, [p.b(), wn.b()], [t1f.b()])
                    stt(t2f.ap, p[:, 256:512], m0c[:, tcx:tcx + 1], wn.ap, ALU.mult, ALU.mult, [p.b(), wn.b(), m0c.b()], [t2f.b()])
                    tt("dve", hp[:, tcx, :], t1f.ap, t2f.ap, ALU.add, [t1f.b(), t2f.b()], [hp.b(tcx * 256, (tcx + 1) * 256)])
                    tt("dve", hm[:, tcx, :], t1f.ap, t2f.ap, ALU.subtract, [t1f.b(), t2f.b()], [hm.b(tcx * 256, (tcx + 1) * 256)])
                    yield
            ws, wv = load_win_piece(l, 0, 768)
            gfilt = filt_gen()
            for c in range(6):
                for (t0, t1) in TT:
                    p = proj_chunk(ws, wv, c * 128, 128, t0, t1)
                    cp("act", upad[:, 1 + t0:1 + t1], p.ap, [p.b()], [upad.b()])
                widx = [k * 6 + c for k in range(3)]
                if c < 2:
                    conv3(l, "hy_conv", widx, widx, x0T[:, c, :], [x0T.b(c * NT, (c + 1) * NT)])
                elif c < 4:
                    conv3(l, "hy_conv", widx, widx, x1T[:, c - 2, :], [x1T.b((c - 2) * NT, (c - 1) * NT)])
                else:
                    cc = c - 4
                    conv3(l, "hy_conv", widx, widx, convo.ap, [convo.b()])
                    tt("dve", zT[:, cc, :], x1T[:, cc, :], convo.ap, ALU.mult,
                       [x1T.b(cc * NT, (cc + 1) * NT), convo.b()], [zT.b(cc * NT, (cc + 1) * NT)])
                    cp("act", z16T[:, cc, :], zT[:, cc, :], [zT.b(cc * NT, (cc + 1) * NT)], [z16T.b(cc * NT, (cc + 1) * NT)])
                for _ in range(2):
                    gfilt = step(gfilt)
                if l == 0:
                    gmod0[0] = step(gmod0[0])
            stop("hyconv")
            for hh in range(2):
                p = nps()
                pb = p.ap.bitcast(BF16)
                for q in range(4):
                    tcx = hh * 4 + q
                    for cc in range(2):
                        o0 = q * 256 + cc * 128
                        tr(pb[:, o0:o0 + 128], z16T[:, cc, tcx * 128:(tcx + 1) * 128], ident16.ap,
                           [z16T.b(), ident16.b()], [p.b()])
                cp("dve", ztok[:, hh * 4:hh * 4 + 4, :], pb.rearrange("p (a b) -> p a b", a=4), [p.b()], [ztok.b()])
            stop("ztr")
            drain(gfilt)
            stop("filt")
            for j in range(8):
                dr = dring[j % 2]
                drv = dr.ap.rearrange("p c (k m) -> p c k m", k=8)
                dma("sp", dr.ap, din["fwdB"][j].rearrange("p c k m -> p c (k m)"), writes=[dr.b()])
                pk, pz = nps(), nps()
                for half, (mat, rhs_t) in enumerate(((0, hp), (1, hm))):
                    for kc in range(8):
                        mm(pk[:, half * 256:(half + 1) * 256], drv[:, mat, kc, :], rhs_t[:, kc, :], kc == 0, kc == 7,
                           [dr.b(), rhs_t.b()], [pk.b(half * 256, (half + 1) * 256)])
                for half in range(2):
                    for kc in range(8):
                        mm(pz[:, half * 256:(half + 1) * 256], drv[:, half, kc, :], ztok[:, kc, :], kc == 0, kc == 7,
                           [dr.b(), ztok.b()], [pz.b(half * 256, (half + 1) * 256)])
                cp("act", Ksb.ap, pk.ap, [pk.b()], [Ksb.b()])
                tt("dve", t1f.ap, pz[:, 0:256], Ksb[:, 0:256], ALU.mult, [pz.b(), Ksb.b()], [t1f.b()])
                tt("dve", t2f.ap, pz[:, 256:512], Ksb[:, 256:512], ALU.mult, [pz.b(), Ksb.b()], [t2f.b()])
                tt("dve", Yt[:, j, 0:256], t1f.ap, t2f.ap, ALU.subtract, [t1f.b(), t2f.b()], [Yt.b(j * 512, j * 512 + 256)])
                tt("dve", t1f.ap, pz[:, 0:256], Ksb[:, 256:512], ALU.mult, [pz.b(), Ksb.b()], [t1f.b()])
                tt("dve", t2f.ap, pz[:, 256:512], Ksb[:, 0:256], ALU.mult, [pz.b(), Ksb.b()], [t2f.b()])
                tt("dve", Yt[:, j, 256:512], t1f.ap, t2f.ap, ALU.add, [t1f.b(), t2f.b()], [Yt.b(j * 512 + 256, (j + 1) * 512)])
            stop("spec")
            acc = [[psb[4 + 2 * a_ + b_] for b_ in range(2)] for a_ in range(2)]
            for j in range(8):
                dr = dring[j % 2]
                dma("sp", dr.ap, din["invB"][j], writes=[dr.b()])
                for cc in range(2):
                    for ti, (t0, t1) in enumerate(TT):
                        pa = acc[cc][ti]
                        mm(pa.ap, Yt[:, j, cc * 128:(cc + 1) * 128], dr[:, 0, t0:t1], j == 0, False,
                           [Yt.b(j * 512, (j + 1) * 512), dr.b()], [pa.b()])
                        mm(pa.ap, Yt[:, j, 256 + cc * 128:256 + (cc + 1) * 128], dr[:, 1, t0:t1], False, j == 7,
                           [Yt.b(j * 512, (j + 1) * 512), dr.b()], [pa.b()])
            for cc in range(2):
                for ti, (t0, t1) in enumerate(TT):
                    pa = acc[cc][ti]
                    tf = ring("tmpf", tmpf)
                    stt(tf.ap, zT[:, cc, t0:t1], vcol(l, "hy_bias", cc), pa.ap, ALU.mult, ALU.add,
                        [zT.b(cc * NT + t0, cc * NT + t1), vecT[l].b(), pa.b()], [tf.b()])
                    tt("dve", yT[:, cc, t0:t1], tf.ap, x0T[:, cc, t0:t1], ALU.mult,
                       [tf.b(), x0T.b(cc * NT + t0, cc * NT + t1)], [yT.b(cc * NT + t0, cc * NT + t1)])
            dbg("yhy%d" % l, yT, [128, 8, NT])
            stop("hy")

            bgT, cgT = G[1], G[2]
            ws, wv = load_win_piece(l, 768, 768)
            for c in range(6):
                for (t0, t1) in TT:
                    p = proj_chunk(ws, wv, c * 128, 128, t0, t1)
                    if c < 2:
                        cp("act", bgT[:, c, t0:t1], p.ap, [p.b()], [bgT.b(c * NT + t0, c * NT + t1)])
                    elif c < 4:
                        cp("act", cgT[:, c - 2, t0:t1], p.ap, [p.b()], [cgT.b((c - 2) * NT + t0, (c - 2) * NT + t1)])
                    else:
                        cc = c - 4
                        tt("dve", upad[:, 1 + t0:1 + t1], p.ap, cgT[:, cc, t0:t1], ALU.mult,
                           [p.b(), cgT.b(cc * NT + t0, cc * NT + t1)], [upad.b()])
                if c >= 4:
                    cc = c - 4
                    widx = [k * 2 + cc for k in range(3)]
                    conv3(l, "sc_conv", widx, [18 + w_ for w_ in widx], convo.ap, [convo.b()])
                    tt("dve", yT[:, 2 + cc, :], convo.ap, bgT[:, cc, :], ALU.mult,
                       [convo.b(), bgT.b(cc * NT, (cc + 1) * NT)], [yT.b((2 + cc) * NT, (3 + cc) * NT)])
            dbg("ysc%d" % l, yT, [128, 8, NT])
            stop("sc")

            car = Carver()
            cqT = G[0]
            car1 = Carver(g_h[1], "G1", 8192)
            car2 = Carver(g_h[2], "G2", 8192)
            ckvT = car1.get([128, NK], F32)
            esc = car1.get([128, 80], F32)
            rot16 = car1.get([32, 96], BF16)
            ida16 = car1.get([32, 96], BF16)
            cqn16 = car2.get([128, 2, NT], BF16)
            wq16 = car2.get([128, 2, 768], BF16)
            ctxs = car.get([128, 2, 160], F32)
            kpeT = car.get([32, NK], F32)
            ckvn16 = car.get([128, NK], BF16)
            kpe16 = car.get([32, NK], BF16)
            V16 = car.get([128, 10, 768], BF16)
            wqr16 = car.get([128, 2, 768], BF16)
            wka16 = car.get([128, 8, 96], BF16)
            wv16 = car.get([128, 8, 64], BF16)
            SRk = T(stage.ap[0:96, :], stage.key, 0, 4, 1024)
            qT16 = [car.get([100, NT], BF16) for _ in range(2)]
            kT16 = [car.get([100, NK], BF16) for _ in range(2)]
            PT16 = [car.get([128, 512], BF16) for _ in range(5)]
            rD = [car.get([128, 512], F32) for _ in range(2)]

            dma("sp", ctxs[:, :, 0:128], din["cckv"][l].rearrange("(a p) n -> p a n", p=128), writes=[ctxs.b()])
            dma("sp", ctxs[:, :, 128:160], din["ckpe"][l].rearrange("(a p) n -> p a n", p=128), writes=[ctxs.b()])
            ws, wv = load_win_piece(l, 1536, 416)
            wso = wslot()
            wov = wso.ap.rearrange("p (a b) -> p a b", a=8)
            dma("pool", wov, din["w_out"][l].rearrange("(kc p) n -> p kc n", p=128), writes=[wso.b()])
            dma("sp", rot16.ap, din["rot"], writes=[rot16.b()])
            dma("sp", ida16.ap, din["ida"], writes=[ida16.b()])
            dma("pool", wq16.ap, din["w_uq"][l].rearrange("(kc p) n -> p kc n", p=128), writes=[wq16.b()])
            v1 = V16.ap.rearrange("p k (j t d) -> p k j t d", j=4, t=3, d=64)[:, :, :, 1, :]
            P.op("pool", lambda e: e.memset(v1, 1.0), writes=[V16.b()])
            memset("pool", wka16, 0.0)
            wsrc = din["w_ukv"][l].rearrange("k (h t d) -> k h t d", h=8, t=2, d=64)
            dma("pool", wka16[:, :, 0:64], wsrc[:, :, 0, :], writes=[wka16.b()])
            dma("pool", wv16.ap, wsrc[:, :, 1, :], writes=[wv16.b()])
            memset("pool", wqr16, 0.0)
            for kc in range(2):
                src = wq16[:, kc, :].rearrange("p (h g a i) -> p h g a i", h=8, g=6, a=2, i=8)
                dst = wqr16[:, kc, :].rearrange("p (h g a i) -> p h g a i", h=8, g=6, a=2, i=8)
                tsm(dst[:, :, 4:6, 0, :], src[:, :, 4:6, 1, :], -1.0, [wq16.b()], [wqr16.b()])
                cp("dve", dst[:, :, 4:6, 1, :], src[:, :, 4:6, 0, :], [wq16.b()], [wqr16.b()])
            for i in range(2):
                dma("sp", qT16[i][96:100, :], din["qm"], writes=[qT16[i].b()])
                dma("sp", kT16[i][96:100, :], din["km"], writes=[kT16[i].b()])
            for (t0, t1) in TT:
                for c in range(2):
                    p = proj_chunk(ws, wv, c * 128, 128, t0, t1)
                    cp("act", cqT[:, c, t0:t1], p.ap, [p.b()], [cqT.b(c * NT + t0, c * NT + t1)])
                p = proj_chunk(ws, wv, 256, 128, t0, t1)
                cp("act", ckvT[:, t0:t1], p.ap, [p.b()], [ckvT.b(t0, t1)])
                p = proj_chunk(ws, wv, 384, 32, t0, t1)
                cp("act", kpeT[:, t0:t1], p[0:32, :], [p.b()], [kpeT.b(t0, t1)])
            for hh in range(2):
                p = nps()
                p2 = nps()
                for q in range(4):
                    tcx = hh * 4 + q
                    tr(p[:, q * 128:(q + 1) * 128], ckvT[:, tcx * 128:(tcx + 1) * 128], ident.ap,
                       [ckvT.b(0, NT), ident.b()], [p.b(q * 128, (q + 1) * 128)])
                    tr(p2[:, q * 32:(q + 1) * 32], kpeT[:, tcx * 128:(tcx + 1) * 128], ident[0:32, 0:32],
                       [kpeT.b(0, NT), ident.b()], [p2.b(q * 32, (q + 1) * 32)])
                cp("dve", stage[:, 0:512], p.ap, [p.b()], [stage.b()])
                cp("dve", stage[:, 512:640], p2[:, 0:128], [p2.b()], [stage.b()])
                dma("sp", nckv_out[l].rearrange("(tc p) n -> p tc n", p=128)[:, hh * 4:hh * 4 + 4, :],
                    stage[:, 0:512].rearrange("p (a b) -> p a b", a=4), reads=[stage.b()])
                dma("sp", nkpe_out[l].rearrange("(tc p) n -> p tc n", p=128)[:, hh * 4:hh * 4 + 4, :],
                    stage[:, 512:640].rearrange("p (a b) -> p a b", a=4), reads=[stage.b()])
            p = nps()
            for a in range(2):
                tr(p[:, a * 128:(a + 1) * 128], ctxs[:, a, 0:128], ident.ap, [ctxs.b(), ident.b()], [p.b(a * 128, (a + 1) * 128)])
                tr(p[0:32, 256 + a * 128:256 + (a + 1) * 128], ctxs[:, a, 128:160], ident.ap, [ctxs.b(), ident.b()],
                   [p.b(256 + a * 128, 256 + (a + 1) * 128)])
            cp("dve", ckvT[:, NT:NK], p[:, 0:256], [p.b()], [ckvT.b(NT, NK)])
            cp("dve", kpeT[:, NT:NK], p[0:32, 256:512], [p.b()], [kpeT.b(NT, NK)])
            for (t0, t1) in TT:
                pn = nps()
                for c in range(2):
                    s_ = ring("sq", sq16)
                    act(s_.ap, cqT[:, c, t0:t1], AF.Square, [cqT.b(c * NT + t0, c * NT + t1)], [s_.b()])
                    mm(pn.ap, ones16.ap, s_.ap, c == 0, c == 1, [ones16.b(), s_.b()], [pn.b()])
                rs = ring("rstd", rstd)
                rstd_from_psum(pn, 128, 256, rs)
                for c in range(2):
                    stt(cqn16[:, c, t0:t1], cqT[:, c, t0:t1], vcol(l, "g_q", c), rs.ap, ALU.mult, ALU.mult,
                        [cqT.b(c * NT + t0, c * NT + t1), vecT[l].b(), rs.b()], [cqn16.b(c * NT + t0, c * NT + t1)])
            for (k0, k1) in KT:
                n = k1 - k0
                pn = nps()
                s_ = ring("sq", sq16)
                act(s_[:, 0:n], ckvT[:, k0:k1], AF.Square, [ckvT.b(k0, k1)], [s_.b()])
                mm(pn[:, 0:n], ones16.ap, s_[:, 0:n], True, True, [ones16.b(), s_.b()], [pn.b()])
                rs = ring("rstd", rstd)
                act(rs[:, 0:n], pn[:, 0:n], AF.Ln, [pn.b(), epsc.b()], [rs.b()], scale=1.0 / 128, bias=epsc[:, 0:1])
                act(rs[:, 0:n], rs[:, 0:n], AF.Exp, [rs.b()], [rs.b()], scale=-0.5)
                stt(ckvn16[:, k0:k1], ckvT[:, k0:k1], vcol(l, "g_kv"), rs[:, 0:n], ALU.mult, ALU.mult,
                    [ckvT.b(k0, k1), vecT[l].b(), rs.b()], [ckvn16.b(k0, k1)])
            cp("dve", kpe16.ap, kpeT.ap, [kpeT.b()], [kpe16.b()])
            for kc in range(10):
                p = nps()
                mm(p.ap, ckvn16[:, kc * 128:(kc + 1) * 128], wv16.ap.rearrange("p h d -> p (h d)"), True, True,
                   [ckvn16.b(kc * 128, (kc + 1) * 128), wv16.b()], [p.b()])
                vdst = V16[:, kc, :].rearrange("p (j t d) -> p j t d", j=4, t=3, d=64)
                vsrc = p.ap.rearrange("p (j t d) -> p j t d", j=4, t=2, d=64)
                cp("act", vdst[:, :, 0, :], vsrc[:, :, 0, :], [p.b()], [V16.b(kc * 768, (kc + 1) * 768)])
                cp("act", vdst[:, :, 2, :], vsrc[:, :, 1, :], [p.b()], [V16.b(kc * 768, (kc + 1) * 768)])
            for (t0, t1) in TT:
                p = nps()
                mm(p[0:96, :], rot16.ap, kpe16[:, t0:t1], True, True, [rot16.b(), kpe16.b(t0, t1)], [p.b()])
                stt(SRk[:, t0:t1], p[0:96, :], vcol(l, "gsw_k", 0, 96), sinf[:, t0:t1], ALU.mult, ALU.mult,
                    [p.b(), vecT[l].b(), sinf.b()], [SRk.b(t0, t1)])
            stop("atprep")
            gq, gk, gswq = vcol(l, "g_qh", 0, 96), vcol(l, "g_kh", 0, 96), vcol(l, "gsw_q", 0, 96)
            rrmod[0] = 5
            pss = psb[5]

            def prep(h):
                kT = kT16[h % 2]
                qT = qT16[h % 2]
                e0 = h * 10
                for ki, (k0, k1) in enumerate(KT):
                    n = k1 - k0
                    p = nps()
                    mm(p[0:96, 0:n], wka16[:, h, :], ckvn16[:, k0:k1], True, False, [wka16.b(), ckvn16.b(k0, k1)], [p.b()])
                    mm(p[0:96, 0:n], ida16.ap, kpe16[:, k0:k1], False, True, [ida16.b(), kpe16.b(k0, k1)], [p.b()])
                    s_ = ring("sq", sq16)
                    act(s_[0:96, 0:n], p[0:96, 0:n], AF.Square, [p.b()], [s_.b()])
                    if ki < 2:
                        tf = ring("tmpf", tmpf)
                        stt(tf[0:96, :], p[0:96, :], gk, cosf[:, k0:k1], ALU.mult, ALU.mult,
                            [p.b(), vecT[l].b(), cosf.b()], [tf.b()])
                        tt("dve", kT[0:96, k0:k1], tf[0:96, :], SRk[:, k0:k1], ALU.add, [tf.b(), SRk.b(k0, k1)], [kT.b(k0, k1)])
                    else:
                        tsm(kT[0:96, k0:k1], p[0:96, 0:n], gk, [p.b(), vecT[l].b()], [kT.b(k0, k1)])
                    yield
                    for kk in range(n // 128):
                        col = (k0 // 128) + kk
                        mm(pss[:, col:col + 1], s_[0:96, kk * 128:(kk + 1) * 128], ones16[0:96, 0:1], True, True,
                           [s_.b(), ones16.b()], [pss.b(col, col + 1)])
                    yield
                act(esc[:, e0:e0 + 10], pss[:, 0:10], AF.Ln, [pss.b(0, 10), epsc.b()], [esc.b(e0, e0 + 10)],
                    scale=1.0 / 96, bias=epsc[:, 0:1])
                act(esc[:, e0:e0 + 10], esc[:, e0:e0 + 10], AF.Exp, [esc.b(e0, e0 + 10), lnsc.b()], [esc.b(e0, e0 + 10)],
                    scale=-0.5, bias=lnsc[:, 0:1])
                yield
                for (t0, t1) in TT:
                    pq, pr = nps(), nps()
                    for kc in range(2):
                        mm(pq[0:96, :], wq16[:, kc, h * 96:(h + 1) * 96], cqn16[:, kc, t0:t1], kc == 0, kc == 1,
                           [wq16.b(), cqn16.b(kc * NT + t0, kc * NT + t1)], [pq.b()])
                    for kc in range(2):
                        mm(pr[0:96, :], wqr16[:, kc, h * 96:(h + 1) * 96], cqn16[:, kc, t0:t1], kc == 0, kc == 1,
                           [wqr16.b(), cqn16.b(kc * NT + t0, kc * NT + t1)], [pr.b()])
                    s_ = ring("sq", sq16)
                    act(s_[0:96, :], pq[0:96, :], AF.Square, [pq.b()], [s_.b()])
                    tf, tf2 = ring("tmpf", tmpf), ring("tmpf", tmpf)
                    stt(tf[0:96, :], pq[0:96, :], gq, cosf[:, t0:t1], ALU.mult, ALU.mult, [pq.b(), vecT[l].b(), cosf.b()], [tf.b()])
                    stt(tf2[0:96, :], pr[0:96, :], gswq, sinf[:, t0:t1], ALU.mult, ALU.mult, [pr.b(), vecT[l].b(), sinf.b()], [tf2.b()])
                    yield
                    pn = nps()
                    mm(pn[0:96, :], ones16[0:96, 0:96], s_[0:96, :], True, True, [ones16.b(), s_.b()], [pn.b()])
                    tt("dve", tf[0:96, :], tf[0:96, :], tf2[0:96, :], ALU.add, [tf.b(), tf2.b()], [tf.b()])
                    yield
                    rs = ring("rstd", rstd)
                    rstd_from_psum(pn, 96, 96, rs)
                    tt("dve", qT[0:96, t0:t1], tf[0:96, :], rs[0:96, :], ALU.mult, [tf.b(), rs.b()], [qT.b(t0, t1)])
                    yield

            fin = [None]

            def scores(h, g, g2):
                kT = kT16[h % 2]
                qT = qT16[h % 2]
                e0 = h * 10
                pair, half = h // 2, (h % 2) * 64
                for ti, (t0, t1) in enumerate(TT):
                    po = psb[6 + (2 * h + ti) % 2]
                    oth = 64 - half

                    def smm(kc):
                        pS_ = nps_s()
                        mm(pS_.ap, kT[:, kc * 128:(kc + 1) * 128], qT[:, t0:t1], True, True,
                           [kT.b(kc * 128, (kc + 1) * 128), qT.b(t0, t1)], [pS_.b()])
                        return pS_
                    pq_ = [smm(0), smm(1)]
                    for kc in range(12):
                        if kc < 10:
                            if kc + 2 < 10:
                                pq_.append(smm(kc + 2))
                            pS = pq_[kc]
                            pt = PT16[kc % 5]
                            act(pt.ap, pS.ap, AF.Exp, [pS.b(), esc.b(e0 + kc, e0 + kc + 1)], [pt.b()],
                                scale=esc[:, e0 + kc:e0 + kc + 1])
                        if kc == 1 and fin[0] is not None:
                            fin[0]()
                            fin[0] = None
                        if kc >= 2:
                            kv = kc - 2
                            pt = PT16[kv % 5]
                            mm(po.ap, V16[:, kv, pair * 192 + half:pair * 192 + half + 128], pt.ap, kv == 0, kv == 9,
                               [V16.b(kv * 768, (kv + 1) * 768), pt.b()], [po.b()])
                        g = step(g)
                        g2[0] = step(g2[0])

                    def _fin(h=h, t0=t0, t1=t1, half=half, oth=oth, pair=pair, po=po):
                        r_ = rD[h % 2]
                        recip(r_[oth:oth + 64, :], po[oth:oth + 64, :], [po.b()], [r_.b()])
                        tt("dve", yT[half:half + 64, 4 + pair, t0:t1], po[half:half + 64, :], r_[oth:oth + 64, :], ALU.mult,
                           [po.b(), r_.b()], [yT.b((4 + pair) * NT + t0, (4 + pair) * NT + t1)])
                    fin[0] = _fin
                return g

            pool_lo[0], pool_n[0] = 3, 2
            drain(prep(0))
            def chain_mod():
                if l == 0 and gmod0[0] is not None:
                    holder0[0] = ws
                    yield from gmod0[0]
                    gmod0[0] = None
                if l + 1 < DEPTH and "one_layer" not in debug:
                    yield from mod_gen(l + 1, [ws])
            g2 = [chain_mod()]
            for h in range(8):
                g = prep(h + 1) if h < 7 else None
                g = scores(h, g, g2)
                drain(g)
            if fin[0] is not None:
                fin[0]()
                fin[0] = None
            drain(g2[0])
            pool_lo[0] = None
            dbg("yat%d" % l, yT, [128, 8, NT])
            stop("at")

            for (t0, t1) in TT:
                for (c0, c1) in ((0, 2), (2, 4), (4, 8)):
                    pn = nps()
                    for c in range(c0, c1):
                        s_ = ring("sq", sq16)
                        act(s_.ap, yT[:, c, t0:t1], AF.Square, [yT.b(c * NT + t0, c * NT + t1)], [s_.b()])
                        mm(pn.ap, ones16.ap, s_.ap, c == c0, c == c1 - 1, [ones16.b(), s_.b()], [pn.b()])
                    rs = ring("rstd", rstd)
                    rstd_from_psum(pn, 128, 128 * (c1 - c0), rs)
                    for c in range(c0, c1):
                        stt(yT[:, c, t0:t1], yT[:, c, t0:t1], vcol(l, "g_grp", c), rs.ap, ALU.mult, ALU.mult,
                            [yT.b(c * NT + t0, c * NT + t1), vecT[l].b(), rs.b()], [yT.b(c * NT + t0, c * NT + t1)])
            for (t0, t1) in TT:
                for dc in range(8):
                    p = nps()
                    for kc in range(8):
                        mm(p.ap, wov[:, kc, dc * 128:(dc + 1) * 128], yT[:, kc, t0:t1], kc == 0, kc == 7,
                           [wso.b(), yT.b(kc * NT + t0, kc * NT + t1)], [p.b()])
                    stt(xT[:, dc, t0:t1], p.ap, mT[:, 16 + dc:17 + dc], xT[:, dc, t0:t1], ALU.mult, ALU.add,
                        [p.b(), mT.b(16, 24), xT.b(dc * NT + t0, dc * NT + t1)], [xT.b(dc * NT + t0, dc * NT + t1)])
            dbg("xmid%d" % l, xT, [128, 8, NT])
            stop("wo")

            norm_mod(l, 1)
            car = Carver()
            actT = car.get([128, 22, NT], BF16)
            nblk = [(b * 512, min(512, DFF - b * 512)) for b in range(6)]
            for (f0, nf) in nblk:
                ws = wslot()
                wg = ws.ap[:, 0:8 * nf].rearrange("p (a b) -> p a b", a=8)
                wu = ws.ap[:, 4096:4096 + 8 * nf].rearrange("p (a b) -> p a b", a=8)
                dma("pool", wg, din["w_ff_gate"][l][:, f0:f0 + nf].rearrange("(kc p) n -> p kc n", p=128), writes=[ws.b(0, 4096)])
                dma("pool", wu, din["w_ff_up"][l][:, f0:f0 + nf].rearrange("(kc p) n -> p kc n", p=128), writes=[ws.b(4096, 8192)])
                for (t0, t1) in TT:
                    for fc in range(nf // 128):
                        f = f0 // 128 + fc
                        pg, pu = nps(), nps()
                        for kc in range(8):
                            mm(pg.ap, wg[:, kc, fc * 128:(fc + 1) * 128], hT[:, kc, t0:t1], kc == 0, kc == 7,
                               [ws.b(0, 4096), hT.b(kc * NT + t0, kc * NT + t1)], [pg.b()])
                        for kc in range(8):
                            mm(pu.ap, wu[:, kc, fc * 128:(fc + 1) * 128], hT[:, kc, t0:t1], kc == 0, kc == 7,
                               [ws.b(4096, 8192), hT.b(kc * NT + t0, kc * NT + t1)], [pu.b()])
                        tf = ring("tmpf", tmpf)
                        act(tf.ap, pg.ap, AF.Silu, [pg.b()], [tf.b()])
                        tt("dve", actT[:, f, t0:t1], tf.ap, pu.ap, ALU.mult, [tf.b(), pu.b()], [actT.b(f * NT + t0, f * NT + t1)])
            for db in range(4):
                ws = wslot()
                wd = ws.ap[:, 0:2 * 22 * 128].rearrange("p (d f m) -> p d f m", d=2, f=22)
                for dd in range(2):
                    dc = db * 2 + dd
                    dma("pool", wd[:, dd], din["w_ff_down"][l][:, dc * 128:(dc + 1) * 128].rearrange("(f p) m -> p f m", p=128),
                        writes=[ws.b(dd * 2816, (dd + 1) * 2816)])
                for dd in range(2):
                    dc = db * 2 + dd
                    for (t0, t1) in TT:
                        p = nps()
                        for f in range(22):
                            mm(p.ap, wd[:, dd, f, :], actT[:, f, t0:t1], f == 0, f == 21,
                               [ws.b(dd * 2816, (dd + 1) * 2816), actT.b(f * NT + t0, f * NT + t1)], [p.b()])
                        stt(xT[:, dc, t0:t1], p.ap, mT[:, 40 + dc:41 + dc], xT[:, dc, t0:t1], ALU.mult, ALU.add,
                            [p.b(), mT.b(40, 48), xT.b(dc * NT + t0, dc * NT + t1)], [xT.b(dc * NT + t0, dc * NT + t1)])
            dbg("xout%d" % l, xT, [128, 8, NT])

        m0c = sb([128, 8], F32, "m0c")
        dma("sp", m0c.ap, din["m0"], writes=[m0c.b()])
        nlayers = DEPTH if "one_layer" not in debug else 1
        holder0 = [wring[0]]
        gmod0 = [mod_gen(0, holder0)]
        for tc in range(8):
            for _ in range(3):
                gmod0[0] = step(gmod0[0])
            load_x(tc)
        wri[0] = 1
        try:
            for l in range(nlayers):
                layer(l)
        except _Stop:
            pass

        for tc in range(8):
            st_ = stage if tc % 2 == 0 else stage2
            for hh in range(2):
                p = nps()
                for q in range(4):
                    dc = hh * 4 + q
                    tr(p[:, q * 128:(q + 1) * 128], xT[:, dc, tc * 128:(tc + 1) * 128], ident.ap,
                       [xT.b(dc * NT + tc * 128, dc * NT + (tc + 1) * 128), ident.b()], [p.b(q * 128, (q + 1) * 128)])
                cp("act" if hh else "dve", st_[:, hh * 512:(hh + 1) * 512], p.ap, [p.b()], [st_.b(hh * 512, (hh + 1) * 512)])
            dma("sp", y_out[tc * 128:(tc + 1) * 128, :], st_.ap, reads=[st_.b()])

        P.emit(nc)
    return nc, dbg_out


_NC_CACHE = {}


def _in_maps(inputs):
    consts = {True: _core_consts(True), False: _core_consts(False)}
    f32 = np.float32
    maps = []
    for r in range(8):
        is_p = r < 4
        m = {}
        if is_p:
            m["x"] = np.ascontiguousarray(inputs["x_prompt"][4 * r:4 * r + 4].reshape(NT, D)).astype(f32, copy=False)
            cvec = np.asarray(inputs["c_ctx"], f32)
            m["cckv"] = np.zeros((2, 256, 128), f32)
            m["ckpe"] = np.zeros((2, 256, 32), f32)
        else:
            b = r - 4
            m["x"] = np.ascontiguousarray(inputs["x_sample"][b]).astype(f32, copy=False)
            cvec = np.asarray(inputs["c"][b], f32)
            m["cckv"] = np.ascontiguousarray(inputs["cache_ckv"][b]).astype(f32, copy=False)
            m["ckpe"] = np.ascontiguousarray(inputs["cache_kpe"][b]).astype(f32, copy=False)
        m["cvT"] = np.ascontiguousarray(cvec.reshape(8, 128).T)
        for n in W_NAMES:
            m[n] = np.ascontiguousarray(np.asarray(inputs[n], f32))
        m.update(consts[is_p])
        maps.append(m)
    return maps


def kernel(**inputs):
    inputs = {k: np.asarray(v) for k, v in inputs.items()}
    if "nc" not in _NC_CACHE:
        _NC_CACHE["nc"] = build_nc()[0]
    nc = _NC_CACHE["nc"]
    res = run_bass_kernel_spmd(nc, _in_maps(inputs), core_ids=list(range(8)))
    rs = res.results
    y_prompt = np.concatenate([rs[r]["y"].reshape(4, 256, D) for r in range(4)], axis=0).astype(np.float32)
    y_sample = np.stack([rs[r]["y"] for r in range(4, 8)], axis=0).astype(np.float32)
    new_ckv = np.concatenate([rs[r]["nckv"].reshape(2, 4, 256, 128).transpose(1, 0, 2, 3) for r in range(4)], axis=0)
    new_kpe = np.concatenate([rs[r]["nkpe"].reshape(2, 4, 256, 32).transpose(1, 0, 2, 3) for r in range(4)], axis=0)
    return (y_prompt, y_sample, np.ascontiguousarray(new_ckv, dtype=np.float32),
            np.ascontiguousarray(new_kpe, dtype=np.float32))
```
